# Optimizing a Trainium2 kernel written in Bass

```python
import math
import jax, jax.numpy as jnp
from jax import lax
import numpy as np

D_MODEL = 2048
BATCH = 4
SEQ = 2048
DEPTH = 2
DEC_BATCH = 128
DEC_SEQ = 1
PAST_LEN = 16384
PAGE_SIZE = 128

GROUP_WIDTH = D_MODEL // 4
MIX_WIDTH = 4 * GROUP_WIDTH
N_GROUP_HEADS = 4
HEAD_DIM = GROUP_WIDTH // N_GROUP_HEADS
GLA_DK = HEAD_DIM // 2
GLA_QK = N_GROUP_HEADS * GLA_DK
GLA_LOWRANK = 16
GLA_TAU = 16.0
GLA_CHUNK = 64
SC_WIDTH = 3
CF_WIDTH = 31
SG_CHUNK = 128
D_FF = (8 * D_MODEL + 3 * 256 - 1) // (3 * 256) * 256
EPS = 1e-6
IN_SIZES = (GROUP_WIDTH, GROUP_WIDTH, GROUP_WIDTH,
            GLA_QK, GLA_QK, GROUP_WIDTH, GROUP_WIDTH, GLA_LOWRANK,
            2 * GROUP_WIDTH, 2 * GROUP_WIDTH)
N_IN = sum(IN_SIZES)

kernel_name = 'hybrid_conv_gla_conformer_sgu_step'


def _split_points():
    pts, acc = [], 0
    for s in IN_SIZES[:-1]:
        acc += s
        pts.append(acc)
    return pts


def rmsnorm(x, g):
    xf = x.astype(jnp.float32)
    y = xf * lax.rsqrt(jnp.mean(xf * xf, axis=-1, keepdims=True) + EPS) * g.astype(jnp.float32)
    return y.astype(x.dtype)


def layernorm(x, g, b):
    xf = x.astype(jnp.float32)
    mu = jnp.mean(xf, axis=-1, keepdims=True)
    xc = xf - mu
    y = xc * lax.rsqrt(jnp.mean(xc * xc, axis=-1, keepdims=True) + EPS)
    return (y * g.astype(jnp.float32) + b.astype(jnp.float32)).astype(x.dtype)


def causal_dwconv(u, past, w):
    ext = jnp.concatenate([past.astype(u.dtype), u], axis=1)
    y = lax.conv_general_dilated(ext, w[:, None, :].astype(u.dtype), window_strides=(1,),
                                 padding='VALID', dimension_numbers=('NWC', 'WIO', 'NWC'),
                                 feature_group_count=u.shape[-1])
    return y, ext[:, -(w.shape[0] - 1):]


def gla_chunked(q, k, v, log_a, s0, chunk):
    B, L, H, _ = q.shape
    n = L // chunk

    def to_chunks(t):
        return t.reshape(B, n, chunk, H, t.shape[-1]).transpose(1, 0, 3, 2, 4)

    qc, kc, vc, ac = (to_chunks(t) for t in (q, k, v, log_a))
    mask = jnp.tril(jnp.ones((chunk, chunk), dtype=bool))[:, :, None]

    def step(S, inp):
        qi, ki, vi, ai = inp
        b = jnp.cumsum(ai, axis=2)
        diff = b[:, :, :, None, :] - b[:, :, None, :, :]
        decay = jnp.exp(jnp.where(mask, diff, -jnp.inf))
        att = jnp.einsum('bhid,bhjd,bhijd->bhij', qi, ki, decay)
        o = jnp.einsum('bhij,bhjv->bhiv', att, vi) + jnp.einsum('bhid,bhdv->bhiv', qi * jnp.exp(b), S)
        b_last = b[:, :, -1:, :]
        S = jnp.exp(b_last[:, :, 0, :])[..., None] * S + jnp.einsum('bhjd,bhjv->bhdv', ki * jnp.exp(b_last - b), vi)
        return S, o

    S, o = lax.scan(step, s0, (qc, kc, vc, ac))
    o = o.transpose(1, 0, 3, 2, 4).reshape(B, L, H, v.shape[-1])
    return o, S


def mixer_groups(h, past_a, s0, past_c, w_in, conv_a_w, gla_a2, gla_a_bias, gla_norm,
                 conv_c_w, conv_c_b, ln_c_g, ln_c_b, ln_d_g, ln_d_b, sg_w, sg_b, w_o):
    B, L, _ = h.shape
    f32 = jnp.float32
    z = h @ w_in
    a_x, a_b, a_c, g_q, g_k, g_v, g_r, g_lr, c_glu, d_uv = jnp.split(z, _split_points(), axis=-1)

    ya, new_a = causal_dwconv(a_c * a_x, past_a, conv_a_w)
    ya = a_b * ya

    q = (g_q.reshape(B, L, N_GROUP_HEADS, GLA_DK) * (GLA_DK ** -0.5)).astype(f32)
    k = g_k.reshape(B, L, N_GROUP_HEADS, GLA_DK).astype(f32)
    v = g_v.reshape(B, L, N_GROUP_HEADS, HEAD_DIM).astype(f32)
    log_a = jax.nn.log_sigmoid((g_lr @ gla_a2 + gla_a_bias).astype(f32)) / GLA_TAU
    log_a = log_a.reshape(B, L, N_GROUP_HEADS, GLA_DK)
    o, S = gla_chunked(q, k, v, log_a, s0.astype(f32), math.gcd(L, GLA_CHUNK))
    o = o * lax.rsqrt(jnp.mean(o * o, axis=-1, keepdims=True) + EPS)
    o = o * gla_norm.reshape(N_GROUP_HEADS, HEAD_DIM).astype(f32)
    yb = o.reshape(B, L, GROUP_WIDTH).astype(h.dtype) * jax.nn.silu(g_r)

    c_in = c_glu[..., :GROUP_WIDTH] * jax.nn.sigmoid(c_glu[..., GROUP_WIDTH:])
    cc, new_c = causal_dwconv(c_in, past_c, conv_c_w)
    yc = jax.nn.silu(layernorm(cc + conv_c_b, ln_c_g, ln_c_b))

    d = jax.nn.gelu(d_uv)
    u, vd = d[..., :GROUP_WIDTH], d[..., GROUP_WIDTH:]
    vd = layernorm(vd, ln_d_g, ln_d_b)
    c = min(L, SG_CHUNK)
    vr = vd.reshape(B, L // c, c, N_GROUP_HEADS, HEAD_DIM)
    ws = jnp.where(jnp.tril(jnp.ones((c, c), dtype=bool))[None], sg_w[:, :c, :c], 0.0)
    sv = jnp.einsum('hij,bnjhd->bnihd', ws.astype(vr.dtype), vr) + sg_b[:, :c].T[:, :, None]
    yd = u * sv.reshape(B, L, GROUP_WIDTH)

    y = jnp.concatenate([ya, yb, yc, yd], axis=-1) @ w_o
    return y, new_a, S, new_c, vd[:, L - c:]


def swiglu(x, w_gate, w_up, w_down):
    return (jax.nn.silu(x @ w_gate) * (x @ w_up)) @ w_down


def setup_inputs(seed: int = 0) -> dict:
    key = jax.random.key(seed)
    ks = jax.random.split(key, 32)
    nrm = jax.random.normal
    f32 = jnp.float32
    G, H = GROUP_WIDTH, N_GROUP_HEADS
    return {
        'x_prompt': nrm(ks[0], (BATCH, SEQ, D_MODEL), f32),
        'x_sample': nrm(ks[1], (DEC_BATCH, DEC_SEQ, D_MODEL), f32),
        'state_conv_a': nrm(ks[2], (DEPTH, DEC_BATCH, SC_WIDTH - 1, G), f32),
        'state_gla': 0.5 * nrm(ks[3], (DEPTH, DEC_BATCH, H, GLA_DK, HEAD_DIM), f32),
        'state_conv_c': nrm(ks[4], (DEPTH, DEC_BATCH, CF_WIDTH - 1, G), f32),
        'norm_mix': 1.0 + 0.01 * nrm(ks[5], (DEPTH, D_MODEL), f32),
        'w_in': nrm(ks[6], (DEPTH, D_MODEL, N_IN), f32) * D_MODEL ** -0.5,
        'conv_a_w': nrm(ks[7], (DEPTH, SC_WIDTH, G), f32) * SC_WIDTH ** -0.5,
        'gla_a2': nrm(ks[8], (DEPTH, GLA_LOWRANK, GLA_QK), f32) * GLA_LOWRANK ** -0.5,
        'gla_a_bias': 0.1 * nrm(ks[9], (DEPTH, GLA_QK), f32),
        'gla_norm': 1.0 + 0.01 * nrm(ks[10], (DEPTH, G), f32),
        'conv_c_w': nrm(ks[11], (DEPTH, CF_WIDTH, G), f32) * CF_WIDTH ** -0.5,
        'conv_c_b': 0.02 * nrm(ks[12], (DEPTH, G), f32),
        'ln_c_g': 1.0 + 0.01 * nrm(ks[13], (DEPTH, G), f32),
        'ln_c_b': 0.02 * nrm(ks[14], (DEPTH, G), f32),
        'ln_d_g': 1.0 + 0.01 * nrm(ks[15], (DEPTH, G), f32),
        'ln_d_b': 0.02 * nrm(ks[16], (DEPTH, G), f32),
        'sg_w': nrm(ks[17], (DEPTH, H, SG_CHUNK, SG_CHUNK), f32) * SG_CHUNK ** -0.5,
        'sg_b': 1.0 + 0.1 * nrm(ks[18], (DEPTH, H, SG_CHUNK), f32),
        'w_o': nrm(ks[19], (DEPTH, MIX_WIDTH, D_MODEL), f32) * MIX_WIDTH ** -0.5,
        'norm_ffn': 1.0 + 0.01 * nrm(ks[20], (DEPTH, D_MODEL), f32),
        'w_gate': nrm(ks[21], (DEPTH, D_MODEL, D_FF), f32) * D_MODEL ** -0.5,
        'w_up': nrm(ks[22], (DEPTH, D_MODEL, D_FF), f32) * D_MODEL ** -0.5,
        'w_down': nrm(ks[23], (DEPTH, D_FF, D_MODEL), f32) * D_FF ** -0.5,
        'norm_final': 1.0 + 0.01 * nrm(ks[24], (D_MODEL,), f32),
    }


def reference(x_prompt, x_sample, state_conv_a, state_gla, state_conv_c,
              norm_mix, w_in, conv_a_w, gla_a2, gla_a_bias, gla_norm,
              conv_c_w, conv_c_b, ln_c_g, ln_c_b, ln_d_g, ln_d_b, sg_w, sg_b, w_o,
              norm_ffn, w_gate, w_up, w_down, norm_final):

    def trunk(x, pa, ps, pc):
        na, ns, nc, nv = [], [], [], []
        for l in range(DEPTH):
            h = rmsnorm(x, norm_mix[l])
            y, a_new, s_new, c_new, v_new = mixer_groups(
                h, pa[l], ps[l], pc[l], w_in[l], conv_a_w[l], gla_a2[l], gla_a_bias[l], gla_norm[l],
                conv_c_w[l], conv_c_b[l], ln_c_g[l], ln_c_b[l], ln_d_g[l], ln_d_b[l], sg_w[l], sg_b[l], w_o[l])
            x = x + y
            x = x + swiglu(rmsnorm(x, norm_ffn[l]), w_gate[l], w_up[l], w_down[l])
            na.append(a_new.astype(state_conv_a.dtype))
            ns.append(s_new.astype(state_gla.dtype))
            nc.append(c_new.astype(state_conv_c.dtype))
            nv.append(v_new)
        return rmsnorm(x, norm_final), jnp.stack(na), jnp.stack(ns), jnp.stack(nc), jnp.stack(nv)

    zp_a = [jnp.zeros((BATCH, SC_WIDTH - 1, GROUP_WIDTH), x_prompt.dtype)] * DEPTH
    zp_s = [jnp.zeros((BATCH, N_GROUP_HEADS, GLA_DK, HEAD_DIM), jnp.float32)] * DEPTH
    zp_c = [jnp.zeros((BATCH, CF_WIDTH - 1, GROUP_WIDTH), x_prompt.dtype)] * DEPTH
    y_prompt, conv_a_prompt, gla_prompt, conv_c_prompt, sg_v_prompt = trunk(x_prompt, zp_a, zp_s, zp_c)

    y_sample, conv_a_sample, gla_sample, conv_c_sample, sg_v_sample = trunk(
        x_sample,
        [state_conv_a[l] for l in range(DEPTH)],
        [state_gla[l] for l in range(DEPTH)],
        [state_conv_c[l] for l in range(DEPTH)])

    return (y_prompt, y_sample, conv_a_prompt, conv_a_sample, gla_prompt, gla_sample,
            conv_c_prompt, conv_c_sample, sg_v_prompt, sg_v_sample)
```

```python
import numpy as np
from contextlib import ExitStack
import concourse.bass as bass
import concourse.mybir as mybir
from concourse.bass_utils import run_bass_kernel_spmd

F32 = mybir.dt.float32
BF16 = mybir.dt.bfloat16
AF = mybir.ActivationFunctionType
ALU = mybir.AluOpType
AX = mybir.AxisListType

NCORES = 8
D = 2048
TP = 1024
NSMP = 16
T = TP + NSMP
TB = [(0, 512), (512, 1024), (1024, 1040)]
TBF = [(0, 352), (352, 704), (704, 1040)]
DFF = 5632
NIN = 5136
EPS = 1e-6
NSLOT = 3
C_AX, C_AB, C_AC, C_Q, C_K, C_V, C_R, C_LR, C_CA, C_CG, C_DU, C_DV = (
    0, 512, 1024, 1536, 1792, 2048, 2560, 3072, 3088, 3600, 4112, 4624)
PV_NMIX, PV_NFFN = 0, 16
PV_CAW = 32
PV_GNORM = 44
PV_CCW = 48
PV_CCB = 172
PV_LCG = 176
PV_LCB = 180
PV_W00 = 184
PV_B0 = 188
PVL = 192
PV_NFIN = 2 * PVL
NPV = 2 * PVL + 16
XS_S, XS_A, XS_C, XSN = 0, 256, 264, 384


def I(name, *a, **k):
    return lambda e: getattr(e, name)(*a, **k)


class Sched:
    ENGS = ("pe", "act", "dve", "pool", "sp")
    NDMA = 8

    def __init__(self, nc, stack):
        self.nc = nc
        self.ops = []
        self.last_w = {}
        self.readers = {}
        self.sems = {e: stack.enter_context(nc.semaphore("s_" + e)) for e in self.ENGS}
        self.dsems = {q: [stack.enter_context(nc.semaphore("d_%s%d" % (q, i))) for i in range(self.NDMA)]
                      for q in ("sp", "pool")}
        self.dcount = {q: 0 for q in self.dsems}
        self.dlast = {q: [None] * self.NDMA for q in self.dsems}
        self.custom = []
        self.dead = False

    def _deps(self, reads, writes):
        deps = set()
        for k in reads:
            w = self.last_w.get(k)
            if w is not None:
                deps.add(w)
        for k in writes:
            w = self.last_w.get(k)
            if w is not None:
                deps.add(w)
            deps.update(self.readers.get(k, ()))
        return deps

    def _commit(self, oid, reads, writes):
        for k in reads:
            self.readers.setdefault(k, []).append(oid)
        for k in writes:
            self.last_w[k] = oid
            self.readers[k] = []

    def op(self, eng, fn, reads=(), writes=(), tok=None):
        if self.dead:
            return None
        psr = [k for k in reads if isinstance(k, tuple) and k[0] == "ps"]
        if psr:
            reads = [k for k in reads if k not in psr]
            writes = list(writes) + psr
        deps = self._deps(reads, writes)
        oid = len(self.ops)
        self.ops.append(dict(eng=eng, fn=fn, deps=deps, dma=None, tok=tok))
        self._commit(oid, reads, writes)
        if tok is not None:
            self.custom.append(oid)
        return oid

    def dma(self, q, fn, reads=(), writes=()):
        if self.dead:
            return None
        deps = self._deps(reads, writes)
        i = self.dcount[q]
        self.dcount[q] += 1
        slot = i % self.NDMA
        val = 16 * (i // self.NDMA + 1)
        prev = self.dlast[q][slot]
        if prev is not None:
            deps.add(prev)
        oid = len(self.ops)
        self.dlast[q][slot] = oid
        self.ops.append(dict(eng=q, fn=fn, deps=deps, dma=(self.dsems[q][slot], val), tok=None))
        self._commit(oid, reads, writes)
        return oid

    def barrier(self):
        if self.dead:
            return
        engs = ("pe", "act", "dve", "sp")
        last = {}
        for oid in range(len(self.ops) - 1, -1, -1):
            o = self.ops[oid]
            if o["dma"] is None and o["tok"] is None and o["eng"] in engs and o["eng"] not in last and o["fn"] is not None:
                last[o["eng"]] = oid
            if len(last) == 3:
                break
        dm = [x for x in self.dlast["sp"] if x is not None]
        for e in engs:
            deps = set(v for k, v in last.items() if k != e)
            deps.update(dm)
            self.ops.append(dict(eng=e, fn=None, deps=deps, dma=None, tok=None))

    def emit(self, block):
        ops = self.ops
        needed = [False] * len(ops)
        for o in ops:
            for d in o["deps"]:
                needed[d] = True
        tok = [None] * len(ops)
        kind = [None] * len(ops)
        cnt = {e: 0 for e in self.ENGS}
        for i, o in enumerate(ops):
            if o["dma"] is not None:
                tok[i] = o["dma"]
                kind[i] = 16
            elif o["tok"] is not None:
                tok[i] = o["tok"]
                kind[i] = 0
            elif needed[i] and o["fn"] is not None:
                cnt[o["eng"]] += 1
                tok[i] = (self.sems[o["eng"]], cnt[o["eng"]])
                kind[i] = 1
        per = {e: [] for e in self.ENGS}
        seen = {e: {} for e in self.ENGS}
        for i, o in enumerate(ops):
            e = o["eng"]
            waits = []
            for d in sorted(o["deps"]):
                od = ops[d]
                if od["dma"] is None and od["tok"] is None and od["eng"] == e and e == "pe":
                    continue
                t = tok[d]
                if t is None:
                    continue
                sem, val = t
                if seen[e].get(id(sem), 0) >= val:
                    continue
                seen[e][id(sem)] = val
                waits.append((sem, val))
            per[e].append((waits, o["fn"], tok[i], kind[i]))
        finals = []
        for q in self.dlast:
            for oid in self.dlast[q]:
                if oid is not None:
                    finals.append(tok[oid])
        for e in self.ENGS:
            if e != "sp" and cnt[e] > 0:
                finals.append((self.sems[e], cnt[e]))

        def run(engname, eng):
            for waits, fn, t, kd in per[engname]:
                for sem, val in waits:
                    eng.wait_ge(sem, val)
                if fn is None:
                    continue
                ins = fn(eng)
                if t is not None:
                    if kd == 0:
                        ins.then_inc(t[0])
                    else:
                        ins.then_inc(t[0], kd)
            if engname == "sp":
                for sem, val in finals:
                    eng.wait_ge(sem, val)

        block.tensor(lambda e: run("pe", e))
        block.scalar(lambda e: run("act", e))
        block.vector(lambda e: run("dve", e))
        block.gpsimd(lambda e: run("pool", e))
        block.sync(lambda e: run("sp", e))


class Arena:
    def __init__(self, t, n):
        self.t, self.n, self.off, self.cnt = t, n, 0, 0

    def alloc(self, free_shape, name):
        n = int(np.prod(free_shape))
        n = (n + 7) // 8 * 8
        assert self.off + n <= self.n, ("arena overflow", name, self.off, n, self.n)
        ap = self.t[:, self.off:self.off + int(np.prod(free_shape))]
        key = ("ar", id(self), self.off)
        self.off += n
        self.cnt += 1
        if len(free_shape) == 2:
            ap = ap.rearrange("p (a b) -> p a b", a=free_shape[0])
        elif len(free_shape) == 3:
            ap = ap.rearrange("p (a b c) -> p a b c", a=free_shape[0], b=free_shape[1])
        return ap, key

    def mark(self):
        return self.off

    def release(self, m):
        self.off = m


class _Stop(Exception):
    pass


def build_nc(ncores=NCORES, kstop=0):
    nc = bass.Bass("TRN2", target_bir_lowering=False)

    def din(name, shape):
        return nc.dram_tensor(name, list(shape), F32, kind="ExternalInput").ap()

    def dout(name, shape):
        return nc.dram_tensor(name, list(shape), F32, kind="ExternalOutput").ap()

    xT_in = din("xT_in", [128, 16, T])
    sca = din("sca", [2, 128, 4, 2, NSMP])
    sgl = din("sgl", [2, NSMP, 4, 64, 128])
    scc = din("scc", [2, 128, 4, NSMP, 30])
    pvec = din("pvec", [128, NPV])
    bc_in = din("bc_in", [2, 128, 1280 + 512])
    sgwT = din("sgwT", [2, 128, 4, 128])
    a2_in = din("a2_in", [2, 16, 256])
    consts = din("consts", [128, 256])
    wsel_in = din("wsel", [128, 8])
    w_in = din("w_in", [2, D, NIN])
    w_o = din("w_o", [2, D, D])
    w_gate = din("w_gate", [2, D, DFF])
    w_up = din("w_up", [2, D, DFF])
    w_down = din("w_down", [2, DFF, D])

    yT = dout("yT", [128, 16, T])
    o_ca_p = dout("o_ca_p", [2, 128, 4, 2])
    o_ca_s = dout("o_ca_s", [2, 128, 4, 2, NSMP])
    o_gl_p = dout("o_gl_p", [2, 256, 128])
    o_gl_s = dout("o_gl_s", [2, NSMP, 4, 64, 128])
    o_cc_p = dout("o_cc_p", [2, 128, 4, 30])
    o_cc_s = dout("o_cc_s", [2, 128, 4, NSMP, 30])
    o_sv_p = dout("o_sv_p", [2, 128, 512])
    o_sv_s = dout("o_sv_s", [2, NSMP, 512])
    cin = [nc.dram_tensor("cin%d" % l, [128, XSN], F32) for l in range(2)]
    cout = [nc.dram_tensor("cout%d" % l, [2 * 128, XSN], F32) for l in range(2)]

    with ExitStack() as st:
        S = Sched(nc, st)
        ccsem = [st.enter_context(nc.semaphore("cc%d" % l)) for l in range(2)]

        def sb(name, shape, dt):
            return st.enter_context(nc.sbuf_tensor(name, list(shape), dt))

        xT = sb("xT", [128, 16, T], F32)
        hT = sb("hT", [128, 16, T], BF16)
        mixT = sb("mixT", [128, 16, T], BF16)
        ring = [sb("ring%d" % i, [128, 4096], BF16) for i in range(NSLOT)]
        pv = sb("pv", [128, NPV], F32)
        cst = sb("cst", [128, 256], F32)
        cstb = sb("cstb", [128, 256], BF16)
        onesB = sb("onesB", [128, 128], BF16)
        wsel = sb("wselt", [128, 8], F32)
        bct = sb("bct", [128, 1280 + 512], F32)
        wsT = sb("wsT", [128, 4, 128], BF16)
        a2t = sb("a2t", [16, 256], F32)
        xs = sb("xs", [128, XSN], F32)
        xsel = sb("xsel", [128, XSN], F32)
        NAF, NAB = 5300, 8900
        af_t = sb("arenaF", [128, NAF], F32)
        ab_t = sb("arenaB", [128, NAB], BF16)
        AFa, ABa = Arena(af_t, NAF), Arena(ab_t, NAB)
        ps = [st.enter_context(nc.psum_tensor("ps%d" % i, [128, 512], F32)) for i in range(8)]
        block = st.enter_context(nc.Block())
        identF = cst[:, 0:128]
        triuF = cst[:, 128:256]
        identB = cstb[:, 0:128]
        triuB = cstb[:, 128:256]

        pbc = [0]

        reserved = set()

        def newps():
            while True:
                b = pbc[0] % 8
                pbc[0] += 1
                if b not in reserved:
                    return b

        wcount = [0]

        def loadw(src, nk, ncols):
            i = wcount[0] % NSLOT
            wcount[0] += 1
            view = ring[i][:, 0:nk * ncols].rearrange("p (k n) -> p k n", k=nk)
            S.dma("pool", I("dma_start", out=view, in_=src.rearrange("(k p) n -> p k n", p=128)),
                  writes=[("ring", i)])
            return view, ("ring", i)

        def mm_group(lhsT_k, rhs_k, nk, rkeys, tbs=TB, m=128):
            banks = [newps() for _ in tbs]
            for k in range(nk):
                for bi, (t0, t1) in enumerate(tbs):
                    S.op("pe", I("matmul", ps[banks[bi]][0:m, 0:t1 - t0], lhsT=lhsT_k(k), rhs=rhs_k(k, t0, t1),
                                 start=(k == 0), stop=(k == nk - 1)),
                         reads=rkeys, writes=[("ps", banks[bi])])
            return banks

        hkeys = [("hT", c) for c in range(16)]
        mkeys = [("mix", c) for c in range(16)]

        S.dma("sp", I("dma_start", out=pv[:], in_=pvec), writes=["pv"])
        S.dma("sp", I("dma_start", out=cst[:], in_=consts), writes=["cst"])
        S.dma("sp", I("dma_start", out=wsel[:], in_=wsel_in), writes=["wsel"])
        for c in range(16):
            S.dma("sp", I("dma_start", out=xT[:, c, :], in_=xT_in[:, c, :]), writes=[("xT", c)])
        S.op("dve", I("tensor_copy", out=cstb[:], in_=cst[:]), reads=["cst"], writes=["cstb"])
        S.op("dve", I("memset", onesB[:], 1.0), writes=["ones"])

        def rmsnorm_to_hT(gcol0, lname):
            mF, mB = AFa.mark(), ABa.mark()
            sq = [ABa.alloc([T], "sq%d" % i) for i in range(2)]
            rs, rsk = AFa.alloc([T], "rs")
            banks = [newps() for _ in TB]
            for c in range(16):
                sqa, sqk = sq[c % 2]
                S.op("act", I("activation", out=sqa, in_=xT[:, c, :], func=AF.Square), reads=[("xT", c)], writes=[sqk])
                for bi, (t0, t1) in enumerate(TB):
                    S.op("pe", I("matmul", ps[banks[bi]][:, 0:t1 - t0], lhsT=onesB[:], rhs=sqa[:, t0:t1],
                                 start=(c == 0), stop=(c == 15)), reads=[sqk, "ones"], writes=[("ps", banks[bi])])
            for bi, (t0, t1) in enumerate(TB):
                S.op("act", I("activation", out=rs[:, t0:t1], in_=ps[banks[bi]][:, 0:t1 - t0], func=AF.Sqrt,
                              bias=EPS, scale=1.0 / D), reads=[("ps", banks[bi])], writes=[rsk])
            S.op("dve", I("reciprocal", out=rs, in_=rs), reads=[rsk], writes=[rsk])
            return rs, rsk, (mF, mB)

        def norm_apply(rs, rsk, gcol0):
            for c in range(16):
                S.op("dve", I("scalar_tensor_tensor", out=hT[:, c, :], in0=xT[:, c, :], scalar=pv[:, gcol0 + c:gcol0 + c + 1],
                              in1=rs, op0=ALU.mult, op1=ALU.mult), reads=[("xT", c), rsk, "pv"], writes=[("hT", c)])

        def win_cols(l, c0, n):
            return w_in[l, :, c0:c0 + n]

        def chk(n):
            if kstop and n == kstop:
                S.dead = True

        for l in range(2):
            P = l * PVL
            S.dma("sp", I("dma_start", out=bct[:], in_=bc_in[l]), writes=["bct"])
            S.dma("sp", I("dma_start", out=a2t[:], in_=a2_in[l]), writes=["a2t"])
            rs, rsk, mk = rmsnorm_to_hT(P + PV_NMIX, "n1")
            norm_apply(rs, rsk, P + PV_NMIX)
            S.barrier()
            AFa.release(mk[0]); ABa.release(mk[1])
            mL_F, mL_B = AFa.mark(), ABa.mark()
            headA, hak = AFa.alloc([4, 4], "headA")
            extC, eck = ABa.alloc([4, 30 + T], "extC")
            mP_F, mP_B = AFa.mark(), ABa.mark()
            wtmp, wtk = AFa.alloc([4, 128], "wtmp")
            S.dma("sp", I("dma_start", out=wtmp, in_=sgwT[l]), writes=[wtk])
            S.op("dve", I("tensor_tensor", out=wsT[:], in0=wtmp, in1=triuF.unsqueeze(1).to_broadcast([128, 4, 128]),
                          op=ALU.mult), reads=[wtk, "cst"], writes=["wsT"])

            chk(1)
            hisA, hisk = AFa.alloc([4, 2, NSMP], "hisA")
            oas, oask = AFa.alloc([4, 2, NSMP], "oas")
            S.dma("sp", I("dma_start", out=hisA, in_=sca[l]), writes=[hisk])
            tacc = [AFa.alloc([T], "tacc%d" % i) for i in range(2)]
            exts = [AFa.alloc([T + 2], "extA%d" % i) for i in range(2)]
            for j in range(2):
                S.op("dve", I("memset", exts[j][0][:, 0:2], 0.0), writes=[exts[j][1]])
            for hb in range(2):
                sc, kc = loadw(win_cols(l, C_AC + hb * 256, 256), 16, 256)
                for j in range(2):
                    tmpc, tck = tacc[j]
                    bc_ = mm_group(lambda k, s=sc, j=j: s[:, k, j * 128:(j + 1) * 128], lambda k, t0, t1: hT[:, k, t0:t1], 16, [kc] + hkeys)
                    for bi, (t0, t1) in enumerate(TB):
                        S.op("act", I("copy", out=tmpc[:, t0:t1], in_=ps[bc_[bi]][:, 0:t1 - t0]), reads=[("ps", bc_[bi])], writes=[tck])
                sx, kx = loadw(win_cols(l, C_AX + hb * 256, 256), 16, 256)
                for j in range(2):
                    cc = hb * 2 + j
                    tmpc, tck = tacc[j]
                    acc, ack = tacc[j]
                    ext, exk = exts[j]
                    w0 = pv[:, P + PV_CAW + cc * 3 + 0:P + PV_CAW + cc * 3 + 1]
                    w1 = pv[:, P + PV_CAW + cc * 3 + 1:P + PV_CAW + cc * 3 + 2]
                    w2 = pv[:, P + PV_CAW + cc * 3 + 2:P + PV_CAW + cc * 3 + 3]
                    bx = mm_group(lambda k, s=sx, j=j: s[:, k, j * 128:(j + 1) * 128], lambda k, t0, t1: hT[:, k, t0:t1], 16, [kx] + hkeys)
                    for bi, (t0, t1) in enumerate(TB):
                        S.op("dve", I("tensor_tensor", out=ext[:, 2 + t0:2 + t1], in0=ps[bx[bi]][:, 0:t1 - t0], in1=tmpc[:, t0:t1], op=ALU.mult),
                             reads=[("ps", bx[bi]), tck], writes=[exk])
                    S.op("dve", I("tensor_scalar", out=acc[:, 0:TP], in0=ext[:, 2:2 + TP], scalar1=w2, scalar2=None, op0=ALU.mult),
                         reads=[exk, "pv"], writes=[ack])
                    S.op("dve", I("scalar_tensor_tensor", out=acc[:, 0:TP], in0=ext[:, 1:1 + TP], scalar=w1, in1=acc[:, 0:TP], op0=ALU.mult, op1=ALU.add),
                         reads=[exk, ack], writes=[ack])
                    S.op("dve", I("scalar_tensor_tensor", out=acc[:, 0:TP], in0=ext[:, 0:TP], scalar=w0, in1=acc[:, 0:TP], op0=ALU.mult, op1=ALU.add),
                         reads=[exk, ack], writes=[ack])
                    S.op("dve", I("tensor_scalar", out=acc[:, TP:T], in0=ext[:, 2 + TP:2 + T], scalar1=w2, scalar2=None, op0=ALU.mult),
                         reads=[exk], writes=[ack])
                    S.op("dve", I("scalar_tensor_tensor", out=acc[:, TP:T], in0=hisA[:, cc, 1, :], scalar=w1, in1=acc[:, TP:T], op0=ALU.mult, op1=ALU.add),
                         reads=[hisk, ack], writes=[ack])
                    S.op("dve", I("scalar_tensor_tensor", out=acc[:, TP:T], in0=hisA[:, cc, 0, :], scalar=w0, in1=acc[:, TP:T], op0=ALU.mult, op1=ALU.add),
                         reads=[hisk, ack], writes=[ack])
                    S.op("dve", I("tensor_copy", out=xs[:, XS_A + cc * 2:XS_A + cc * 2 + 2], in_=ext[:, TP:TP + 2]), reads=[exk], writes=["xs"])
                    S.op("dve", I("tensor_copy", out=oas[:, cc, 1, :], in_=ext[:, 2 + TP:2 + T]), reads=[exk], writes=[oask])
                    S.op("dve", I("tensor_copy", out=oas[:, cc, 0, :], in_=hisA[:, cc, 1, :]), reads=[hisk], writes=[oask])
                    S.op("dve", I("tensor_copy", out=headA[:, cc, 0:2], in_=acc[:, 0:2]), reads=[ack], writes=[hak])
                sbb, kb = loadw(win_cols(l, C_AB + hb * 256, 256), 16, 256)
                for j in range(2):
                    cc = hb * 2 + j
                    acc, ack = tacc[j]
                    bb = mm_group(lambda k, s=sbb, j=j: s[:, k, j * 128:(j + 1) * 128], lambda k, t0, t1: hT[:, k, t0:t1], 16, [kb] + hkeys)
                    S.op("dve", I("tensor_copy", out=headA[:, cc, 2:4], in_=ps[bb[0]][:, 0:2]), reads=[("ps", bb[0])], writes=[hak])
                    for bi, (t0, t1) in enumerate(TB):
                        S.op("dve", I("tensor_tensor", out=mixT[:, cc, t0:t1], in0=ps[bb[bi]][:, 0:t1 - t0], in1=acc[:, t0:t1], op=ALU.mult),
                             reads=[("ps", bb[bi]), ack], writes=[("mix", cc)])
            S.dma("sp", I("dma_start", out=o_ca_s[l], in_=oas), reads=[oask], writes=["o_ca_s"])
            S.dma("sp", I("dma_start", out=o_ca_p[l], in_=xs[:, XS_A:XS_A + 8].rearrange("p (c r) -> p c r", c=4)), reads=["xs"], writes=["o_ca_p"])
            S.barrier()
            AFa.release(mP_F); ABa.release(mP_B)

            chk(2)
            ocs, ocsk = AFa.alloc([4, NSMP, 30], "ocs")
            hisC, hck = AFa.alloc([4, NSMP, 30], "hisC")
            S.dma("sp", I("dma_start", out=hisC, in_=scc[l]), writes=[hck])
            tmpg, tgk = AFa.alloc([T], "tmpg")
            for hb in range(2):
                sa, ka = loadw(win_cols(l, C_CA + hb * 256, 256), 16, 256)
                sg_, kg = loadw(win_cols(l, C_CG + hb * 256, 256), 16, 256)
                for j in range(2):
                    cc = hb * 2 + j
                    bg = mm_group(lambda k, s=sg_, j=j: s[:, k, j * 128:(j + 1) * 128], lambda k, t0, t1: hT[:, k, t0:t1], 16, [kg] + hkeys)
                    for bi, (t0, t1) in enumerate(TB):
                        S.op("act", I("activation", out=tmpg[:, t0:t1], in_=ps[bg[bi]][:, 0:t1 - t0], func=AF.Sigmoid), reads=[("ps", bg[bi])], writes=[tgk])
                    ba = mm_group(lambda k, s=sa, j=j: s[:, k, j * 128:(j + 1) * 128], lambda k, t0, t1: hT[:, k, t0:t1], 16, [ka] + hkeys)
                    for bi, (t0, t1) in enumerate(TB):
                        S.op("dve", I("tensor_tensor", out=extC[:, cc, 30 + t0:30 + t1], in0=ps[ba[bi]][:, 0:t1 - t0], in1=tmpg[:, t0:t1], op=ALU.mult),
                             reads=[("ps", ba[bi]), tgk], writes=[eck])
                    S.op("dve", I("tensor_tensor", out=xs[:, XS_C + cc * 30:XS_C + cc * 30 + 30], in0=ps[ba[1]][:, 482:512], in1=tmpg[:, TP - 30:TP], op=ALU.mult),
                         reads=[("ps", ba[1]), tgk], writes=["xs"])
                    S.op("dve", I("tensor_tensor", out=ocs[:, cc, :, 29], in0=ps[ba[2]][:, 0:NSMP], in1=tmpg[:, TP:T], op=ALU.mult),
                         reads=[("ps", ba[2]), tgk], writes=[ocsk])
                    S.op("dve", I("tensor_copy", out=ocs[:, cc, :, 0:29], in_=hisC[:, cc, :, 1:30]), reads=[hck], writes=[ocsk])
            S.dma("sp", I("dma_start", out=o_cc_s[l], in_=ocs), reads=[ocsk], writes=["o_cc_s"])
            S.dma("sp", I("dma_start", out=o_cc_p[l], in_=xs[:, XS_C:XS_C + 120].rearrange("p (c r) -> p c r", c=4)), reads=["xs"], writes=["o_cc_p"])
            S.barrier()
            AFa.release(mP_F); ABa.release(mP_B)

            chk(3)
            o_bf = mixT[:, 4:8, :]
            obk = [("mix", 4 + h) for h in range(4)]
            qT = mixT[:, 8:10, :]
            kT = mixT[:, 10:12, :]
            qk = [("mix", 8), ("mix", 9)]
            kk = [("mix", 10), ("mix", 11)]
            qh32, qhk = AFa.alloc([TP], "qhat32")
            qhat = qh32.bitcast(BF16).rearrange("p (a b) -> p a b", a=2)
            ePc, epk = AFa.alloc([2], "ePc")
            Sst, ssk = AFa.alloc([2, 128], "Sst")
            mG_F, mG_B = AFa.mark(), ABa.mark()
            glr, glk = AFa.alloc([T], "glr")
            Sbf, sbk = ABa.alloc([4, 128], "Sbm")
            S.op("dve", I("memset", ePc, 1.0), writes=[epk])
            S.op("dve", I("memset", Sst, 0.0), writes=[ssk])
            S.op("dve", I("memset", Sbf, 0.0), writes=[sbk])
            sq_, kq_ = loadw(win_cols(l, C_Q, 256), 16, 256)
            for hh in range(2):
                b_ = mm_group(lambda k, hh=hh: sq_[:, k, hh * 128:(hh + 1) * 128], lambda k, t0, t1: hT[:, k, t0:t1], 16, [kq_] + hkeys)
                for bi, (t0, t1) in enumerate(TB):
                    S.op("act", I("copy", out=qT[:, hh, t0:t1], in_=ps[b_[bi]][:, 0:t1 - t0]), reads=[("ps", b_[bi])], writes=[qk[hh]])
            sk_, kk_ = loadw(win_cols(l, C_K, 256), 16, 256)
            for hh in range(2):
                b_ = mm_group(lambda k, hh=hh: sk_[:, k, hh * 128:(hh + 1) * 128], lambda k, t0, t1: hT[:, k, t0:t1], 16, [kk_] + hkeys)
                for bi, (t0, t1) in enumerate(TB):
                    S.op("act", I("copy", out=kT[:, hh, t0:t1], in_=ps[b_[bi]][:, 0:t1 - t0]), reads=[("ps", b_[bi])], writes=[kk[hh]])
            slr, klr = loadw(win_cols(l, C_LR, 16), 16, 16)
            b_ = mm_group(lambda k: slr[:, k, 0:16], lambda k, t0, t1: hT[:, k, t0:t1], 16, [klr] + hkeys, m=16)
            for bi, (t0, t1) in enumerate(TB):
                S.op("act", I("copy", out=glr[0:16, t0:t1], in_=ps[b_[bi]][0:16, 0:t1 - t0]), reads=[("ps", b_[bi])], writes=[glk])
            kq, kqk = AFa.alloc([512], "kq")
            bkq = newps()
            for which, (sw, kw) in enumerate(((sk_, kk_), (sq_, kq_))):
                for k in range(16):
                    S.op("pe", I("matmul", ps[bkq][0:NSMP, which * 256:(which + 1) * 256], lhsT=hT[:, k, TP:T], rhs=sw[:, k, :],
                                 start=(k == 0), stop=(k == 15)), reads=[kw, ("hT", k)], writes=[("ps", bkq)])
            S.op("act", I("copy", out=kq[0:NSMP, :], in_=ps[bkq][0:NSMP, :]), reads=[("ps", bkq)], writes=[kqk])
            chk(31)
            sv0, kv0 = loadw(win_cols(l, C_V, 256), 16, 256)
            sv1, kv1 = loadw(win_cols(l, C_V + 256, 256), 16, 256)
            mTF, mTB = AFa.mark(), ABa.mark()
            frames = []
            for p_ in range(2):
                fr = dict(sp=AFa.alloc([256], "sp"), eb=AFa.alloc([2, 128], "eb"), enb=AFa.alloc([2, 128], "enb"), ebl=AFa.alloc([2], "ebl"),
                          khT=AFa.alloc([2, 128], "khT"), vbf=ABa.alloc([512], "vbf"), qt=ABa.alloc([2, 128], "qt"), khtok=ABa.alloc([256], "khtok"),
                          att=ABa.alloc([4, 128], "att"), kt=ABa.alloc([4, 128], "ktm"))
                S.op("dve", I("memset", fr["kt"][0], 0.0), writes=[fr["kt"][1]])
                frames.append(fr)

            def g_stage_a(n):
                fr = frames[n % 2]
                c0 = n * 128
                sp_, spk = fr["sp"]; eb, ebk = fr["eb"]; enb, enk = fr["enb"]; ebl, eblk = fr["ebl"]; khT, khk = fr["khT"]
                vbf, vbk = fr["vbf"]; qt, qtk = fr["qt"]; khtok, khtk = fr["khtok"]; att, atk = fr["att"]; kt, ktk = fr["kt"]
                bl = newps()
                S.op("pe", I("matmul", ps[bl][:, 0:256], lhsT=glr[0:16, c0:c0 + 128], rhs=a2t[:, :], start=True, stop=True),
                     reads=[glk, "a2t"], writes=[("ps", bl)])
                S.op("dve", I("tensor_tensor", out=sp_, in0=ps[bl][:, 0:256], in1=bct[:, 1024:1280], op=ALU.add),
                     reads=[("ps", bl), "bct"], writes=[spk])
                S.op("act", I("activation", out=sp_, in_=sp_, func=AF.Exp, scale=-1.0), reads=[spk], writes=[spk])
                S.op("act", I("activation", out=sp_, in_=sp_, func=AF.Ln, bias=1.0), reads=[spk], writes=[spk])
                bv = newps()
                for half, (sv, kv) in enumerate(((sv0, kv0), (sv1, kv1))):
                    for k in range(16):
                        S.op("pe", I("matmul", ps[bv][:, half * 256:(half + 1) * 256], lhsT=hT[:, k, c0:c0 + 128], rhs=sv[:, k, :],
                                     start=(k == 0), stop=(k == 15)), reads=[kv, ("hT", k)], writes=[("ps", bv)])
                S.op("act", I("copy", out=vbf, in_=ps[bv][:, :]), reads=[("ps", bv)], writes=[vbk])
                bcs = newps()
                for hh in range(2):
                    S.op("pe", I("matmul", ps[bcs][:, hh * 128:(hh + 1) * 128], lhsT=sp_[:, hh * 128:(hh + 1) * 128], rhs=triuF,
                                 start=True, stop=True), reads=[spk, "cst"], writes=[("ps", bcs)])
                S.op("act", I("activation", out=eb, in_=ps[bcs][:, 0:256].rearrange("p (a b) -> p a b", a=2), func=AF.Exp, scale=-1.0 / 16), reads=[("ps", bcs)], writes=[ebk])
                S.op("act", I("activation", out=enb, in_=ps[bcs][:, 0:256].rearrange("p (a b) -> p a b", a=2), func=AF.Exp, scale=1.0 / 16), reads=[("ps", bcs)], writes=[enk])
                S.op("dve", I("tensor_copy", out=ebl, in_=eb[:, :, 127]), reads=[ebk], writes=[eblk])
                for hh in range(2):
                    S.op("dve", I("scalar_tensor_tensor", out=qt[:, hh, :], in0=qT[:, hh, c0:c0 + 128], scalar=0.125, in1=eb[:, hh, :], op0=ALU.mult, op1=ALU.mult),
                         reads=[qk[hh], ebk], writes=[qtk])
                    S.op("dve", I("scalar_tensor_tensor", out=khT[:, hh, :], in0=kT[:, hh, c0:c0 + 128], scalar=ebl[:, hh:hh + 1], in1=enb[:, hh, :], op0=ALU.mult, op1=ALU.mult),
                         reads=[kk[hh], enk, eblk], writes=[khk])
                    S.op("dve", I("tensor_scalar", out=qhat[:, hh, c0:c0 + 128], in0=qt[:, hh, :], scalar1=ePc[:, hh:hh + 1], scalar2=None, op0=ALU.mult),
                         reads=[qtk, epk], writes=[qhk])
                for hp in range(2):
                    r0 = hp * 64
                    S.op("dve", I("tensor_tensor", out=kt[r0:r0 + 64, hp:4:2, :], in0=kT[r0:r0 + 64, :, c0:c0 + 128], in1=enb[r0:r0 + 64, :, :], op=ALU.mult),
                         reads=kk + [enk], writes=[ktk])
                S.op("dve", I("tensor_tensor", out=ePc, in0=ePc, in1=ebl, op=ALU.mult), reads=[epk, eblk, qhk], writes=[epk])
                btr = newps()
                for hh in range(2):
                    S.op("pe", I("transpose", ps[btr][:, hh * 128:(hh + 1) * 128], khT[:, hh, :], identF), reads=[khk, "cst"], writes=[("ps", btr)])
                S.op("act", I("copy", out=khtok, in_=ps[btr][:, 0:256]), reads=[("ps", btr)], writes=[khtk])
                batt = newps()
                for h in range(4):
                    S.op("pe", I("matmul", ps[batt][:, h * 128:(h + 1) * 128], lhsT=kt[:, h, :], rhs=qt[:, h // 2, :], start=True, stop=True),
                         reads=[ktk, qtk], writes=[("ps", batt)])
                S.op("dve", I("tensor_tensor", out=att, in0=ps[batt][:, :].rearrange("p (h i) -> p h i", h=4), in1=triuF.unsqueeze(1).to_broadcast([128, 4, 128]), op=ALU.mult),
                     reads=[("ps", batt), "cst"], writes=[atk])

            def g_stage_b(n):
                fr = frames[n % 2]
                c0 = n * 128
                ebl, eblk = fr["ebl"]; vbf, vbk = fr["vbf"]; qt, qtk = fr["qt"]; khtok, khtk = fr["khtok"]; att, atk = fr["att"]
                bo = newps()
                for h in range(4):
                    hh = h // 2
                    S.op("pe", I("matmul", ps[bo][:, h * 128:(h + 1) * 128], lhsT=vbf[:, h * 128:(h + 1) * 128], rhs=att[:, h, :], start=True, stop=False),
                         reads=[vbk, atk], writes=[("ps", bo)])
                    S.op("pe", I("matmul", ps[bo][:, h * 128:(h + 1) * 128], lhsT=Sbf[:, h, :], rhs=qt[:, hh, :], start=False, stop=True),
                         reads=[sbk, qtk], writes=[("ps", bo)])
                S.op("act", I("copy", out=o_bf[:, :, c0:c0 + 128], in_=ps[bo][:, :].rearrange("p (h i) -> p h i", h=4)), reads=[("ps", bo)], writes=obk)
                bs = newps()
                for h in range(4):
                    hh = h // 2
                    S.op("pe", I("matmul", ps[bs][:, h * 128:(h + 1) * 128], lhsT=khtok[:, hh * 128:(hh + 1) * 128], rhs=vbf[:, h * 128:(h + 1) * 128], start=True, stop=True),
                         reads=[khtk, vbk], writes=[("ps", bs)])
                for h in range(4):
                    p0, hh = (h % 2) * 64, h // 2
                    S.op("dve", I("scalar_tensor_tensor", out=Sst[p0:p0 + 64, hh, :], in0=Sst[p0:p0 + 64, hh, :], scalar=ebl[p0:p0 + 64, hh:hh + 1],
                                  in1=ps[bs][p0:p0 + 64, h * 128:(h + 1) * 128], op0=ALU.mult, op1=ALU.add),
                         reads=[ssk, eblk, ("ps", bs)], writes=[ssk])
                for hp in range(2):
                    r0 = hp * 64
                    S.op("act", I("copy", out=Sbf[r0:r0 + 64, hp:4:2, :], in_=Sst[r0:r0 + 64, :, :]), reads=[ssk], writes=[sbk])

            g_stage_a(0)
            for n in range(8):
                if n + 1 < 8:
                    g_stage_a(n + 1)
                g_stage_b(n)
            for n in (8,):
                if n == 8:
                    chk(32)
                    S.barrier()
                AFa.release(mTF); ABa.release(mTB)
                c0 = n * 128
                m = 128 if n < 8 else NSMP
                sp_, spk = AFa.alloc([256], "sp")
                bl = newps()
                S.op("pe", I("matmul", ps[bl][0:m, 0:256], lhsT=glr[0:16, c0:c0 + m], rhs=a2t[:, :], start=True, stop=True),
                     reads=[glk, "a2t"], writes=[("ps", bl)])
                S.op("dve", I("tensor_tensor", out=sp_[0:m, :], in0=ps[bl][0:m, 0:256], in1=bct[0:m, 1024:1280], op=ALU.add),
                     reads=[("ps", bl), "bct"], writes=[spk])
                S.op("act", I("activation", out=sp_[0:m, :], in_=sp_[0:m, :], func=AF.Exp, scale=-1.0), reads=[spk], writes=[spk])
                S.op("act", I("activation", out=sp_[0:m, :], in_=sp_[0:m, :], func=AF.Ln, bias=1.0), reads=[spk], writes=[spk])
                bv = newps()
                for half, (sv, kv) in enumerate(((sv0, kv0), (sv1, kv1))):
                    for k in range(16):
                        S.op("pe", I("matmul", ps[bv][0:m, half * 256:(half + 1) * 256], lhsT=hT[:, k, c0:c0 + m], rhs=sv[:, k, :],
                                     start=(k == 0), stop=(k == 15)), reads=[kv, ("hT", k)], writes=[("ps", bv)])
                if n < 8:
                    vbf, vbk = ABa.alloc([512], "vbf")
                    S.op("act", I("copy", out=vbf, in_=ps[bv][:, :]), reads=[("ps", bv)], writes=[vbk])
                    bcs = newps()
                    for hh in range(2):
                        S.op("pe", I("matmul", ps[bcs][:, hh * 128:(hh + 1) * 128], lhsT=sp_[:, hh * 128:(hh + 1) * 128], rhs=triuF,
                                     start=True, stop=True), reads=[spk, "cst"], writes=[("ps", bcs)])
                    eb, ebk = AFa.alloc([2, 128], "eb")
                    enb, enk = AFa.alloc([2, 128], "enb")
                    ebl, eblk = AFa.alloc([2], "ebl")
                    S.op("act", I("activation", out=eb, in_=ps[bcs][:, 0:256].rearrange("p (a b) -> p a b", a=2), func=AF.Exp, scale=-1.0 / 16), reads=[("ps", bcs)], writes=[ebk])
                    S.op("act", I("activation", out=enb, in_=ps[bcs][:, 0:256].rearrange("p (a b) -> p a b", a=2), func=AF.Exp, scale=1.0 / 16), reads=[("ps", bcs)], writes=[enk])
                    S.op("dve", I("tensor_copy", out=ebl, in_=eb[:, :, 127]), reads=[ebk], writes=[eblk])
                    qt, qtk = ABa.alloc([2, 128], "qt")
                    khT, khk = AFa.alloc([2, 128], "khT")
                    khtok, khtk = ABa.alloc([256], "khtok")
                    for hh in range(2):
                        S.op("dve", I("scalar_tensor_tensor", out=qt[:, hh, :], in0=qT[:, hh, c0:c0 + 128], scalar=0.125, in1=eb[:, hh, :], op0=ALU.mult, op1=ALU.mult),
                             reads=[qk[hh], ebk], writes=[qtk])
                        S.op("dve", I("scalar_tensor_tensor", out=khT[:, hh, :], in0=kT[:, hh, c0:c0 + 128], scalar=ebl[:, hh:hh + 1], in1=enb[:, hh, :], op0=ALU.mult, op1=ALU.mult),
                             reads=[kk[hh], enk, eblk], writes=[khk])
                        S.op("dve", I("tensor_scalar", out=qhat[:, hh, c0:c0 + 128], in0=qt[:, hh, :], scalar1=ePc[:, hh:hh + 1], scalar2=None, op0=ALU.mult),
                             reads=[qtk, epk], writes=[qhk])
                    for hp in range(2):
                        r0 = hp * 64
                        S.op("dve", I("tensor_tensor", out=kt[r0:r0 + 64, hp:4:2, :], in0=kT[r0:r0 + 64, :, c0:c0 + 128], in1=enb[r0:r0 + 64, :, :], op=ALU.mult),
                             reads=kk + [enk], writes=[ktk])
                    S.op("dve", I("tensor_tensor", out=ePc, in0=ePc, in1=ebl, op=ALU.mult), reads=[epk, eblk, qhk], writes=[epk])
                    btr = newps()
                    for hh in range(2):
                        S.op("pe", I("transpose", ps[btr][:, hh * 128:(hh + 1) * 128], khT[:, hh, :], identF), reads=[khk, "cst"], writes=[("ps", btr)])
                    S.op("act", I("copy", out=khtok, in_=ps[btr][:, 0:256]), reads=[("ps", btr)], writes=[khtk])
                    batt = newps()
                    for h in range(4):
                        S.op("pe", I("matmul", ps[batt][:, h * 128:(h + 1) * 128], lhsT=kt[:, h, :], rhs=qt[:, h // 2, :], start=True, stop=True),
                             reads=[ktk, qtk], writes=[("ps", batt)])
                    att, atk = ABa.alloc([4, 128], "att")
                    S.op("dve", I("tensor_tensor", out=att, in0=ps[batt][:, :].rearrange("p (h i) -> p h i", h=4), in1=triuF.unsqueeze(1).to_broadcast([128, 4, 128]), op=ALU.mult),
                         reads=[("ps", batt), "cst"], writes=[atk])
                    bo = newps()
                    for h in range(4):
                        p0, hh = (h % 2) * 64, h // 2
                        S.op("pe", I("matmul", ps[bo][:, h * 128:(h + 1) * 128], lhsT=vbf[:, h * 128:(h + 1) * 128], rhs=att[:, h, :], start=True, stop=False),
                             reads=[vbk, atk], writes=[("ps", bo)])
                        S.op("pe", I("matmul", ps[bo][:, h * 128:(h + 1) * 128], lhsT=Sbf[:, h, :], rhs=qt[:, hh, :], start=False, stop=True),
                             reads=[sbk, qtk], writes=[("ps", bo)])
                    S.op("act", I("copy", out=o_bf[:, :, c0:c0 + 128], in_=ps[bo][:, :].rearrange("p (h i) -> p h i", h=4)), reads=[("ps", bo)], writes=obk)
                    bs = newps()
                    for h in range(4):
                        hh = h // 2
                        S.op("pe", I("matmul", ps[bs][:, h * 128:(h + 1) * 128], lhsT=khtok[:, hh * 128:(hh + 1) * 128], rhs=vbf[:, h * 128:(h + 1) * 128], start=True, stop=True),
                             reads=[khtk, vbk], writes=[("ps", bs)])
                    for h in range(4):
                        p0, hh = (h % 2) * 64, h // 2
                        S.op("dve", I("scalar_tensor_tensor", out=Sst[p0:p0 + 64, hh, :], in0=Sst[p0:p0 + 64, hh, :], scalar=ebl[p0:p0 + 64, hh:hh + 1],
                                      in1=ps[bs][p0:p0 + 64, h * 128:(h + 1) * 128], op0=ALU.mult, op1=ALU.add),
                             reads=[ssk, eblk, ("ps", bs)], writes=[ssk])
                    for hp in range(2):
                        r0 = hp * 64
                        S.op("act", I("copy", out=Sbf[r0:r0 + 64, hp:4:2, :], in_=Sst[r0:r0 + 64, :, :]), reads=[ssk], writes=[sbk])
                else:
                    vsf, vsk = AFa.alloc([512], "vsf")
                    S.op("act", I("copy", out=vsf[0:NSMP, :], in_=ps[bv][0:NSMP, :]), reads=[("ps", bv)], writes=[vsk])
                    chk(321)
                    btq = newps()
                    for h in range(4):
                        S.op("pe", I("transpose", ps[btq][0:64, h * NSMP:(h + 1) * NSMP], kq[0:NSMP, 256 + h * 64:256 + (h + 1) * 64], identF[0:NSMP, 0:NSMP]),
                             reads=[kqk, "cst"], writes=[("ps", btq)])
                        S.op("pe", I("transpose", ps[btq][0:64, 64 + h * NSMP:64 + (h + 1) * NSMP], sp_[0:NSMP, h * 64:(h + 1) * 64], identF[0:NSMP, 0:NSMP]),
                             reads=[spk, "cst"], writes=[("ps", btq)])
                    chk(3211)
                    qd, qdk = AFa.alloc([4, NSMP], "qd")
                    ad, adk = AFa.alloc([4, NSMP], "ad")
                    S.op("dve", I("tensor_scalar", out=qd[0:64], in0=ps[btq][0:64, 0:64].rearrange("p (h s) -> p h s", h=4), scalar1=0.125, scalar2=None, op0=ALU.mult),
                         reads=[("ps", btq)], writes=[qdk])
                    chk(3212)
                    S.op("act", I("activation", out=ad[0:64], in_=ps[btq][0:64, 64:128].rearrange("p (h s) -> p h s", h=4), func=AF.Exp, scale=-1.0 / 16),
                         reads=[("ps", btq)], writes=[adk])
                    chk(322)
                    bso = [newps(), newps()]
                    reserved.update(bso)
                    Kms = [AFa.alloc([256], "Km%d" % i) for i in range(2)]
                    Sss = [AFa.alloc([4, 128], "Ss%d" % i) for i in range(2)]
                    for s in range(NSMP):
                        Km, kmk = Kms[s % 2]
                        Ss, sskey = Sss[s % 2]
                        S.dma("sp", I("dma_start", out=Ss[0:64], in_=sgl[l, s].rearrange("h d v -> d h v")), writes=[sskey])
                        S.op("dve", I("tensor_scalar", out=Km[0:NSMP], in0=kq[0:NSMP, 0:256], scalar1=identF[0:NSMP, s:s + 1], scalar2=None, op0=ALU.mult),
                             reads=[kqk, "cst"], writes=[kmk])
                        bk_ = newps()
                        for h in range(4):
                            S.op("pe", I("matmul", ps[bk_][0:64, h * 128:(h + 1) * 128], lhsT=Km[0:NSMP, h * 64:(h + 1) * 64], rhs=vsf[0:NSMP, h * 128:(h + 1) * 128], start=True, stop=True),
                                 reads=[kmk, vsk], writes=[("ps", bk_)])
                        for h in range(4):
                            S.op("dve", I("scalar_tensor_tensor", out=Ss[0:64, h, :], in0=Ss[0:64, h, :], scalar=ad[0:64, h, s:s + 1],
                                          in1=ps[bk_][0:64, h * 128:(h + 1) * 128], op0=ALU.mult, op1=ALU.add),
                                 reads=[sskey, adk, ("ps", bk_)], writes=[sskey])
                        for h in range(4):
                            cb = (h % 2) * 256 + s * NSMP
                            S.op("pe", I("matmul", ps[bso[h // 2]][:, cb:cb + NSMP], lhsT=Ss[0:64, h, :], rhs=qd[0:64, h, :], start=True, stop=True),
                                 reads=[sskey, qdk], writes=[("ps", bso[h // 2])])
                        S.dma("sp", I("dma_start", out=o_gl_s[l, s].rearrange("h d v -> d h v"), in_=Ss[0:64]), reads=[sskey], writes=["o_gl_s"])
                    for h in range(4):
                        cb = (h % 2) * 256
                        S.op("dve", I("tensor_copy", out=o_bf[:, h, TP:T], in_=ps[bso[h // 2]][:, cb:cb + 256:17]), reads=[("ps", bso[h // 2])], writes=[obk[h]])
                    reserved.difference_update(bso)
            chk(33)
            S.op("dve", I("tensor_copy", out=xs[:, XS_S:XS_S + 256], in_=Sst.rearrange("p a b -> p (a b)")), reads=[ssk], writes=["xs"])
            S.dma("sp", I("dma_start", out=cin[l].ap(), in_=xs[:]), reads=["xs"], writes=[("cin", l)])
            S.op("pool", lambda e, l=l: e.collective_compute("AllGather", ALU.bypass, replica_groups=[[2 * i, 2 * i + 1] for i in range(ncores // 2)],
                                                         ins=[cin[l].ap().opt()], outs=[cout[l].ap().opt()]),
                 reads=[("cin", l)], writes=[("cout", l)], tok=(ccsem[l], 1))
            S.barrier()
            AFa.release(mG_F); ABa.release(mG_B)

            chk(4)
            ug = mixT[:, 12:16, :]
            ugk = [("mix", 12 + h) for h in range(4)]
            for hb in range(2):
                su, ku = loadw(win_cols(l, C_DU + hb * 256, 256), 16, 256)
                for j in range(2):
                    cc = hb * 2 + j
                    b_ = mm_group(lambda k, j=j: su[:, k, j * 128:(j + 1) * 128], lambda k, t0, t1: hT[:, k, t0:t1], 16, [ku] + hkeys)
                    for bi, (t0, t1) in enumerate(TB):
                        S.op("act", I("activation", out=ug[:, cc, t0:t1], in_=ps[b_[bi]][:, 0:t1 - t0], func=AF.Gelu_apprx_tanh), reads=[("ps", b_[bi])], writes=[ugk[cc]])
            sd0, kd0 = loadw(win_cols(l, C_DV, 256), 16, 256)
            sd1, kd1 = loadw(win_cols(l, C_DV + 256, 256), 16, 256)
            mDt, mDtB = AFa.mark(), ABa.mark()
            dframes = [dict(vg=AFa.alloc([512], "vg"), junk=AFa.alloc([512], "junk"), st=AFa.alloc([8], "st"), svt=AFa.alloc([4, 128], "svt"), vdb=ABa.alloc([512], "vdb"))
                       for _ in range(2)]

            def d_stage_a(n):
                fr = dframes[n % 2]
                c0 = n * 128
                vg, vgk = fr["vg"]; junk, jk = fr["junk"]; st_, stk = fr["st"]; vdb, vdk = fr["vdb"]
                bv = newps()
                for half, (sv, kv) in enumerate(((sd0, kd0), (sd1, kd1))):
                    for k in range(16):
                        S.op("pe", I("matmul", ps[bv][:, half * 256:(half + 1) * 256], lhsT=hT[:, k, c0:c0 + 128], rhs=sv[:, k, :],
                                     start=(k == 0), stop=(k == 15)), reads=[kv, ("hT", k)], writes=[("ps", bv)])
                S.op("act", I("activation", out=vg, in_=ps[bv][:, :], func=AF.Gelu_apprx_tanh), reads=[("ps", bv)], writes=[vgk])
                S.op("act", I("activation", out=junk, in_=vg, func=AF.Square), reads=[vgk], writes=[jk])
                S.op("dve", I("tensor_reduce", out=st_[:, 0:1], in_=vg, axis=AX.X, op=ALU.add), reads=[vgk], writes=[stk])
                S.op("dve", I("tensor_reduce", out=st_[:, 1:2], in_=junk, axis=AX.X, op=ALU.add), reads=[jk, stk], writes=[stk])
                S.op("dve", I("tensor_scalar", out=st_[:, 2:3], in0=st_[:, 0:1], scalar1=1.0 / 512, scalar2=None, op0=ALU.mult), reads=[stk], writes=[stk])
                S.op("dve", I("tensor_tensor", out=st_[:, 3:4], in0=st_[:, 2:3], in1=st_[:, 2:3], op=ALU.mult), reads=[stk], writes=[stk])
                S.op("dve", I("scalar_tensor_tensor", out=st_[:, 4:5], in0=st_[:, 1:2], scalar=1.0 / 512, in1=st_[:, 3:4], op0=ALU.mult, op1=ALU.subtract), reads=[stk], writes=[stk])
                S.op("act", I("activation", out=st_[:, 5:6], in_=st_[:, 4:5], func=AF.Sqrt, bias=EPS, scale=1.0), reads=[stk], writes=[stk])
                S.op("dve", I("reciprocal", out=st_[:, 5:6], in_=st_[:, 5:6]), reads=[stk], writes=[stk])
                S.op("dve", I("tensor_scalar", out=vg, in0=vg, scalar1=st_[:, 2:3], scalar2=st_[:, 5:6], op0=ALU.subtract, op1=ALU.mult), reads=[vgk, stk], writes=[vgk])
                S.op("dve", I("tensor_tensor", out=vg, in0=vg, in1=bct[:, 0:512], op=ALU.mult), reads=[vgk, "bct"], writes=[vgk])
                S.op("dve", I("tensor_tensor", out=vg, in0=vg, in1=bct[:, 512:1024], op=ALU.add), reads=[vgk, "bct"], writes=[vgk])
                S.op("act", I("copy", out=vdb, in_=vg), reads=[vgk], writes=[vdk])
                if n == 7:
                    S.dma("sp", I("dma_start", out=o_sv_p[l], in_=vg), reads=[vgk], writes=["o_sv_p"])

            def d_stage_b(n):
                fr = dframes[n % 2]
                c0 = n * 128
                vdb, vdk = fr["vdb"]; svt, svk = fr["svt"]
                bsv = newps()
                for h in range(4):
                    S.op("pe", I("matmul", ps[bsv][:, h * 128:(h + 1) * 128], lhsT=vdb[:, h * 128:(h + 1) * 128], rhs=wsT[:, h, :], start=True, stop=True),
                         reads=[vdk, "wsT"], writes=[("ps", bsv)])
                S.op("dve", I("tensor_tensor", out=svt, in0=ps[bsv][:, :].rearrange("p (h i) -> p h i", h=4), in1=bct[:, 1280:1792].rearrange("p (h i) -> p h i", h=4), op=ALU.add),
                     reads=[("ps", bsv), "bct"], writes=[svk])
                S.op("dve", I("tensor_tensor", out=mixT[:, 12:16, c0:c0 + 128], in0=svt, in1=ug[:, :, c0:c0 + 128], op=ALU.mult),
                     reads=[svk] + ugk, writes=ugk)

            d_stage_a(0)
            for n in range(8):
                if n + 1 < 8:
                    d_stage_a(n + 1)
                d_stage_b(n)
            for n in (8,):
                if n == 8:
                    S.barrier()
                AFa.release(mDt); ABa.release(mDtB)
                c0 = n * 128
                m = 128 if n < 8 else NSMP
                bv = newps()
                for half, (sv, kv) in enumerate(((sd0, kd0), (sd1, kd1))):
                    for k in range(16):
                        S.op("pe", I("matmul", ps[bv][0:m, half * 256:(half + 1) * 256], lhsT=hT[:, k, c0:c0 + m], rhs=sv[:, k, :],
                                     start=(k == 0), stop=(k == 15)), reads=[kv, ("hT", k)], writes=[("ps", bv)])
                vg, vgk = AFa.alloc([512], "vg")
                junk, jk = AFa.alloc([512], "junk")
                st_, stk = AFa.alloc([8], "st")
                S.op("act", I("activation", out=vg[0:m], in_=ps[bv][0:m, :], func=AF.Gelu_apprx_tanh), reads=[("ps", bv)], writes=[vgk])
                S.op("act", I("activation", out=junk[0:m], in_=vg[0:m], func=AF.Square), reads=[vgk], writes=[jk])
                S.op("dve", I("tensor_reduce", out=st_[0:m, 0:1], in_=vg[0:m], axis=AX.X, op=ALU.add), reads=[vgk], writes=[stk])
                S.op("dve", I("tensor_reduce", out=st_[0:m, 1:2], in_=junk[0:m], axis=AX.X, op=ALU.add), reads=[jk, stk], writes=[stk])
                S.op("dve", I("tensor_scalar", out=st_[0:m, 2:3], in0=st_[0:m, 0:1], scalar1=1.0 / 512, scalar2=None, op0=ALU.mult), reads=[stk], writes=[stk])
                S.op("dve", I("tensor_tensor", out=st_[0:m, 3:4], in0=st_[0:m, 2:3], in1=st_[0:m, 2:3], op=ALU.mult), reads=[stk], writes=[stk])
                S.op("dve", I("scalar_tensor_tensor", out=st_[0:m, 4:5], in0=st_[0:m, 1:2], scalar=1.0 / 512, in1=st_[0:m, 3:4], op0=ALU.mult, op1=ALU.subtract), reads=[stk], writes=[stk])
                S.op("act", I("activation", out=st_[0:m, 5:6], in_=st_[0:m, 4:5], func=AF.Sqrt, bias=EPS, scale=1.0), reads=[stk], writes=[stk])
                S.op("dve", I("reciprocal", out=st_[0:m, 5:6], in_=st_[0:m, 5:6]), reads=[stk], writes=[stk])
                S.op("dve", I("tensor_scalar", out=vg[0:m], in0=vg[0:m], scalar1=st_[0:m, 2:3], scalar2=st_[0:m, 5:6], op0=ALU.subtract, op1=ALU.mult), reads=[vgk, stk], writes=[vgk])
                S.op("dve", I("tensor_tensor", out=vg[0:m], in0=vg[0:m], in1=bct[0:m, 0:512], op=ALU.mult), reads=[vgk, "bct"], writes=[vgk])
                S.op("dve", I("tensor_tensor", out=vg[0:m], in0=vg[0:m], in1=bct[0:m, 512:1024], op=ALU.add), reads=[vgk, "bct"], writes=[vgk])
                if n < 8:
                    vdb, vdk = ABa.alloc([512], "vdb")
                    S.op("act", I("copy", out=vdb, in_=vg), reads=[vgk], writes=[vdk])
                    if n == 7:
                        S.dma("sp", I("dma_start", out=o_sv_p[l], in_=vg), reads=[vgk], writes=["o_sv_p"])
                    bsv = newps()
                    for h in range(4):
                        S.op("pe", I("matmul", ps[bsv][:, h * 128:(h + 1) * 128], lhsT=vdb[:, h * 128:(h + 1) * 128], rhs=wsT[:, h, :], start=True, stop=True),
                             reads=[vdk, "wsT"], writes=[("ps", bsv)])
                    svt, svk = AFa.alloc([4, 128], "svt")
                    S.op("dve", I("tensor_tensor", out=svt, in0=ps[bsv][:, :].rearrange("p (h i) -> p h i", h=4), in1=bct[:, 1280:1792].rearrange("p (h i) -> p h i", h=4), op=ALU.add),
                         reads=[("ps", bsv), "bct"], writes=[svk])
                    S.op("dve", I("tensor_tensor", out=mixT[:, 12:16, c0:c0 + 128], in0=svt, in1=ug[:, :, c0:c0 + 128], op=ALU.mult),
                         reads=[svk] + ugk, writes=ugk)
                else:
                    S.dma("sp", I("dma_start", out=o_sv_s[l], in_=vg[0:NSMP]), reads=[vgk], writes=["o_sv_s"])
                    btv = newps()
                    for h in range(4):
                        S.op("pe", I("transpose", ps[btv][:, h * NSMP:(h + 1) * NSMP], vg[0:NSMP, h * 128:(h + 1) * 128], identF[0:NSMP, 0:NSMP]),
                             reads=[vgk, "cst"], writes=[("ps", btv)])
                    svs, svsk = AFa.alloc([4, NSMP], "svs")
                    for h in range(4):
                        S.op("dve", I("tensor_scalar", out=svs[:, h, :], in0=ps[btv][:, h * NSMP:(h + 1) * NSMP], scalar1=pv[:, P + PV_W00 + h:P + PV_W00 + h + 1],
                                      scalar2=pv[:, P + PV_B0 + h:P + PV_B0 + h + 1], op0=ALU.mult, op1=ALU.add), reads=[("ps", btv), "pv"], writes=[svsk])
                    S.op("dve", I("tensor_tensor", out=mixT[:, 12:16, TP:T], in0=svs, in1=ug[:, :, TP:T], op=ALU.mult),
                         reads=[svsk] + ugk, writes=ugk)
            S.barrier()
            AFa.release(mG_F); ABa.release(mG_B)

            chk(5)
            xr, xrk = AFa.alloc([2, XSN], "xr")
            S.dma("sp", I("dma_start", out=xr, in_=cout[l].ap().rearrange("(r p) x -> p r x", p=128)), reads=[("cout", l)], writes=[xrk])
            S.op("dve", I("tensor_scalar", out=xsel[:], in0=xr[:, 0, :], scalar1=wsel[:, 0:1], scalar2=None, op0=ALU.mult), reads=[xrk, "wsel"], writes=["xsel"])
            for r in range(1, 2):
                S.op("dve", I("scalar_tensor_tensor", out=xsel[:], in0=xr[:, r, :], scalar=wsel[:, r:r + 1], in1=xsel[:], op0=ALU.mult, op1=ALU.add),
                     reads=[xrk, "wsel", "xsel"], writes=["xsel"])
            S.barrier()
            AFa.release(mG_F); ABa.release(mG_B)
            fa, fak = AFa.alloc([4, 2], "fa")
            for cc in range(4):
                w0 = pv[:, P + PV_CAW + cc * 3 + 0:P + PV_CAW + cc * 3 + 1]
                w1 = pv[:, P + PV_CAW + cc * 3 + 1:P + PV_CAW + cc * 3 + 2]
                h0 = xsel[:, XS_A + cc * 2:XS_A + cc * 2 + 1]
                h1 = xsel[:, XS_A + cc * 2 + 1:XS_A + cc * 2 + 2]
                S.op("dve", I("scalar_tensor_tensor", out=fa[:, cc, 0:1], in0=h0, scalar=w0, in1=headA[:, cc, 0:1], op0=ALU.mult, op1=ALU.add), reads=["xsel", hak, "pv"], writes=[fak])
                S.op("dve", I("scalar_tensor_tensor", out=fa[:, cc, 0:1], in0=h1, scalar=w1, in1=fa[:, cc, 0:1], op0=ALU.mult, op1=ALU.add), reads=["xsel", fak], writes=[fak])
                S.op("dve", I("scalar_tensor_tensor", out=fa[:, cc, 1:2], in0=h1, scalar=w0, in1=headA[:, cc, 1:2], op0=ALU.mult, op1=ALU.add), reads=["xsel", hak], writes=[fak])
                S.op("dve", I("tensor_tensor", out=mixT[:, cc, 0:2], in0=fa[:, cc, :], in1=headA[:, cc, 2:4], op=ALU.mult), reads=[fak, hak], writes=[("mix", cc)])

            chk(6)
            sgr = mixT[:, 8:12, :]
            sgk = [("mix", 8 + h) for h in range(4)]
            for hb in range(2):
                sr, kr = loadw(win_cols(l, C_R + hb * 256, 256), 16, 256)
                for j in range(2):
                    cc = hb * 2 + j
                    b_ = mm_group(lambda k, j=j: sr[:, k, j * 128:(j + 1) * 128], lambda k, t0, t1: hT[:, k, t0:t1], 16, [kr] + hkeys)
                    for bi, (t0, t1) in enumerate(TB):
                        S.op("act", I("activation", out=sgr[:, cc, t0:t1], in_=ps[b_[bi]][:, 0:t1 - t0], func=AF.Silu), reads=[("ps", b_[bi])], writes=[sgk[cc]])
            Sib, sibk = ABa.alloc([4, 128], "Sibm")
            S.op("dve", I("memset", Sib, 0.0), writes=[sibk])
            for hp in range(2):
                r0 = hp * 64
                S.op("act", I("copy", out=Sib[r0:r0 + 64, hp:4:2, :], in_=xsel[r0:r0 + 64, XS_S:XS_S + 256].rearrange("p (a b) -> p a b", a=2)), reads=["xsel"], writes=[sibk])
            Sfin, sfk = AFa.alloc([2, 128], "Sfin")
            for hh in range(2):
                S.op("dve", I("scalar_tensor_tensor", out=Sfin[:, hh, :], in0=xsel[:, XS_S + hh * 128:XS_S + (hh + 1) * 128], scalar=ePc[:, hh:hh + 1], in1=Sst[:, hh, :],
                              op0=ALU.mult, op1=ALU.add), reads=["xsel", epk, ssk], writes=[sfk])
            S.dma("sp", I("dma_start", out=o_gl_p[l].rearrange("(hh p) v -> p hh v", p=128), in_=Sfin), reads=[sfk], writes=["o_gl_p"])
            fsets = [(AFa.alloc([512], "of%d" % i), ABa.alloc([512], "osq%d" % i), AFa.alloc([512], "rr%d" % i)) for i in range(3)]
            blocks = [(h, bi) for h in range(4) for bi in range(3)]
            bss_of = {}

            def stage_a(i):
                h, bi = blocks[i]
                hh = h // 2
                t0, t1 = TB[bi]
                w = t1 - t0
                (of, ofk), (osq, osk), (rr, rrk) = fsets[i % 3]
                if bi < 2:
                    bcq = newps()
                    S.op("pe", I("matmul", ps[bcq][:, 0:w], lhsT=Sib[:, h, :], rhs=qhat[:, hh, t0:t1], start=True, stop=True),
                         reads=[sibk, qhk], writes=[("ps", bcq)])
                    S.op("dve", I("tensor_tensor", out=of[:, 0:w], in0=ps[bcq][:, 0:w], in1=o_bf[:, h, t0:t1], op=ALU.add), reads=[("ps", bcq), obk[h]], writes=[ofk])
                else:
                    S.op("dve", I("tensor_copy", out=of[:, 0:w], in_=o_bf[:, h, t0:t1]), reads=[obk[h]], writes=[ofk])
                S.op("act", I("activation", out=osq[:, 0:w], in_=of[:, 0:w], func=AF.Square), reads=[ofk], writes=[osk])
                bss = newps()
                reserved.add(bss)
                bss_of[i] = bss
                S.op("pe", I("matmul", ps[bss][:, 0:w], lhsT=onesB[:], rhs=osq[:, 0:w], start=True, stop=True), reads=[osk, "ones"], writes=[("ps", bss)])

            def stage_b(i):
                h, bi = blocks[i]
                t0, t1 = TB[bi]
                w = t1 - t0
                (of, ofk), (osq, osk), (rr, rrk) = fsets[i % 3]
                bss = bss_of[i]
                S.op("act", I("activation", out=rr[:, 0:w], in_=ps[bss][:, 0:w], func=AF.Sqrt, bias=EPS, scale=1.0 / 128), reads=[("ps", bss)], writes=[rrk])
                reserved.discard(bss)
                S.op("dve", I("reciprocal", out=rr[:, 0:w], in_=rr[:, 0:w]), reads=[rrk], writes=[rrk])
                S.op("dve", I("scalar_tensor_tensor", out=of[:, 0:w], in0=of[:, 0:w], scalar=pv[:, P + PV_GNORM + h:P + PV_GNORM + h + 1], in1=rr[:, 0:w], op0=ALU.mult, op1=ALU.mult),
                     reads=[ofk, rrk, "pv"], writes=[ofk])
                S.op("dve", I("tensor_tensor", out=mixT[:, 4 + h, t0:t1], in0=of[:, 0:w], in1=sgr[:, h, t0:t1], op=ALU.mult), reads=[ofk, sgk[h]], writes=[obk[h]])

            for i in range(len(blocks) + 1):
                if i < len(blocks):
                    stage_a(i)
                if i >= 1:
                    stage_b(i - 1)
            S.barrier()
            AFa.release(mP_F); ABa.release(mP_B)

            chk(7)
            for cc in range(4):
                S.op("act", I("copy", out=extC[:, cc, 0:30], in_=xsel[:, XS_C + cc * 30:XS_C + cc * 30 + 30]), reads=["xsel"], writes=[eck])
            xcs, xcsk = AFa.alloc([4, NSMP], "xcs")
            mCs = AFa.mark()
            hisC, hck = AFa.alloc([4, NSMP, 30], "hisC2")
            accs, ask_ = AFa.alloc([NSMP, 30], "accs")
            S.dma("sp", I("dma_start", out=hisC, in_=scc[l]), writes=[hck])
            for cc in range(4):
                S.op("dve", I("tensor_tensor", out=accs, in0=hisC[:, cc, :, :], in1=pv[:, P + PV_CCW + cc * 31:P + PV_CCW + cc * 31 + 30].unsqueeze(1).to_broadcast([128, NSMP, 30]), op=ALU.mult),
                     reads=[hck, "pv"], writes=[ask_])
                S.op("dve", I("tensor_reduce", out=xcs[:, cc, :], in_=accs, axis=AX.X, op=ALU.add), reads=[ask_], writes=[xcsk])
            S.barrier()
            AFa.release(mCs)
            xc, _ = AFa.alloc([4, T], "xc")
            xck = [("xc", c_) for c_ in range(4)]
            xb = [ABa.alloc([T], "xb%d" % i) for i in range(1)] * 2
            sqb = [ABa.alloc([T], "sqb%d" % i) for i in range(1)] * 2
            dg = [ABa.alloc([128], "dg%d" % i) for i in range(4)]
            b1 = [newps() for _ in TB]
            b2 = [newps() for _ in TB]
            reserved.update(b1); reserved.update(b2)
            for cc in range(4):
                wc = lambda j, cc=cc: pv[:, P + PV_CCW + cc * 31 + j:P + PV_CCW + cc * 31 + j + 1]
                bcol = pv[:, P + PV_CCB + cc:P + PV_CCB + cc + 1]
                cb = [newps(), newps()]
                for j in range(31):
                    dga, dgk = dg[(cc * 31 + j) % 4]
                    S.op("dve", I("tensor_scalar", out=dga, in0=identB, scalar1=wc(j), scalar2=None, op0=ALU.mult), reads=["cstb", "pv"], writes=[dgk])
                    for bi in range(2):
                        t0, t1 = TB[bi]
                        S.op("pe", I("matmul", ps[cb[bi]][:, 0:512], lhsT=dga, rhs=extC[:, cc, t0 + j:t1 + j], start=(j == 0), stop=(j == 30)),
                             reads=[dgk, eck], writes=[("ps", cb[bi])])
                for bi in range(2):
                    t0, t1 = TB[bi]
                    S.op("act", I("activation", out=xc[:, cc, t0:t1], in_=ps[cb[bi]][:, 0:512], func=AF.Identity, bias=bcol, scale=1.0), reads=[("ps", cb[bi]), "pv"], writes=[xck[cc]])
                S.op("dve", I("scalar_tensor_tensor", out=xc[:, cc, TP:T], in0=extC[:, cc, 30 + TP:30 + T], scalar=wc(30), in1=xcs[:, cc, :], op0=ALU.mult, op1=ALU.add),
                     reads=[eck, xcsk], writes=[xck[cc]])
                S.op("dve", I("tensor_scalar", out=xc[:, cc, TP:T], in0=xc[:, cc, TP:T], scalar1=bcol, scalar2=None, op0=ALU.add), reads=[xck[cc], "pv"], writes=[xck[cc]])
                xba, xbk = xb[cc % 2]
                sqa, sqk = sqb[cc % 2]
                S.op("act", I("copy", out=xba, in_=xc[:, cc, :]), reads=[xck[cc]], writes=[xbk])
                S.op("act", I("activation", out=sqa, in_=xc[:, cc, :], func=AF.Square), reads=[xck[cc]], writes=[sqk])
                for bi, (t0, t1) in enumerate(TB):
                    S.op("pe", I("matmul", ps[b1[bi]][:, 0:t1 - t0], lhsT=onesB[:], rhs=xba[:, t0:t1], start=(cc == 0), stop=(cc == 3)), reads=[xbk, "ones"], writes=[("ps", b1[bi])])
                    S.op("pe", I("matmul", ps[b2[bi]][:, 0:t1 - t0], lhsT=onesB[:], rhs=sqa[:, t0:t1], start=(cc == 0), stop=(cc == 3)), reads=[sqk, "ones"], writes=[("ps", b2[bi])])
            reserved.difference_update(b1); reserved.difference_update(b2)
            mean, mnk = AFa.alloc([512], "mean")
            rstd, rsk2 = AFa.alloc([512], "rstd")
            for bi, (t0, t1) in enumerate(TB):
                w = t1 - t0
                S.op("dve", I("tensor_scalar", out=mean[:, 0:w], in0=ps[b1[bi]][:, 0:w], scalar1=1.0 / 512, scalar2=None, op0=ALU.mult), reads=[("ps", b1[bi])], writes=[mnk])
                S.op("dve", I("tensor_tensor", out=rstd[:, 0:w], in0=mean[:, 0:w], in1=mean[:, 0:w], op=ALU.mult), reads=[mnk], writes=[rsk2])
                S.op("dve", I("scalar_tensor_tensor", out=rstd[:, 0:w], in0=ps[b2[bi]][:, 0:w], scalar=1.0 / 512, in1=rstd[:, 0:w], op0=ALU.mult, op1=ALU.subtract),
                     reads=[("ps", b2[bi]), rsk2], writes=[rsk2])
                S.op("act", I("activation", out=rstd[:, 0:w], in_=rstd[:, 0:w], func=AF.Sqrt, bias=EPS, scale=1.0), reads=[rsk2], writes=[rsk2])
                S.op("dve", I("reciprocal", out=rstd[:, 0:w], in_=rstd[:, 0:w]), reads=[rsk2], writes=[rsk2])
                for cc in range(4):
                    S.op("dve", I("tensor_tensor", out=xc[:, cc, t0:t1], in0=xc[:, cc, t0:t1], in1=mean[:, 0:w], op=ALU.subtract), reads=[xck[cc], mnk], writes=[xck[cc]])
                    S.op("dve", I("tensor_tensor", out=xc[:, cc, t0:t1], in0=xc[:, cc, t0:t1], in1=rstd[:, 0:w], op=ALU.mult), reads=[xck[cc], rsk2], writes=[xck[cc]])
                    S.op("act", I("activation", out=mixT[:, 8 + cc, t0:t1], in_=xc[:, cc, t0:t1], func=AF.Silu, scale=pv[:, P + PV_LCG + cc:P + PV_LCG + cc + 1],
                                  bias=pv[:, P + PV_LCB + cc:P + PV_LCB + cc + 1]), reads=[xck[cc], "pv"], writes=[("mix", 8 + cc)])
            S.barrier()
            AFa.release(mL_F); ABa.release(mL_B)

            chk(8)
            for blk in range(8):
                so, ko = loadw(w_o[l, :, blk * 256:(blk + 1) * 256], 16, 256)
                for j in range(2):
                    d = blk * 2 + j
                    b_ = mm_group(lambda k, j=j: so[:, k, j * 128:(j + 1) * 128], lambda k, t0, t1: mixT[:, k, t0:t1], 16, [ko] + mkeys, tbs=TBF)
                    for bi, (t0, t1) in enumerate(TBF):
                        S.op("dve", I("tensor_tensor", out=xT[:, d, t0:t1], in0=ps[b_[bi]][:, 0:t1 - t0], in1=xT[:, d, t0:t1], op=ALU.add),
                             reads=[("ps", b_[bi]), ("xT", d)], writes=[("xT", d)])
            chk(9)
            rs, rsk, mk = rmsnorm_to_hT(P + PV_NFFN, "n2")
            norm_apply(rs, rsk, P + PV_NFFN)
            S.barrier()
            AFa.release(mk[0]); ABa.release(mk[1])
            mFF = AFa.mark()
            sgt = [AFa.alloc([T], "sgt%d" % i) for i in range(2)]
            actb = [(mixT[:, 4 * i:4 * i + 4, :], ("act", i)) for i in range(2)]
            for g in range(11):
                act_, actk = actb[g % 2]
                for half in range(2):
                    sg_, kg = loadw(w_gate[l, :, g * 512 + half * 256:g * 512 + half * 256 + 256], 16, 256)
                    su_, ku = loadw(w_up[l, :, g * 512 + half * 256:g * 512 + half * 256 + 256], 16, 256)
                    for j in range(2):
                        fc = half * 2 + j
                        sga, sgk2 = sgt[fc % 2]
                        bg = mm_group(lambda k, j=j: sg_[:, k, j * 128:(j + 1) * 128], lambda k, t0, t1: hT[:, k, t0:t1], 16, [kg] + hkeys, tbs=TBF)
                        for bi, (t0, t1) in enumerate(TBF):
                            S.op("act", I("activation", out=sga[:, t0:t1], in_=ps[bg[bi]][:, 0:t1 - t0], func=AF.Silu), reads=[("ps", bg[bi])], writes=[sgk2])
                        bu = mm_group(lambda k, j=j: su_[:, k, j * 128:(j + 1) * 128], lambda k, t0, t1: hT[:, k, t0:t1], 16, [ku] + hkeys, tbs=TBF)
                        for bi, (t0, t1) in enumerate(TBF):
                            S.op("dve", I("tensor_tensor", out=act_[:, fc, t0:t1], in0=ps[bu[bi]][:, 0:t1 - t0], in1=sga[:, t0:t1], op=ALU.mult),
                                 reads=[("ps", bu[bi]), sgk2], writes=[actk])
                for half in range(2):
                    sd_, kd = loadw(w_down[l, g * 512:(g + 1) * 512, half * 1024:(half + 1) * 1024], 4, 1024)
                    for j in range(8):
                        d = half * 8 + j
                        b_ = mm_group(lambda k, j=j: sd_[:, k, j * 128:(j + 1) * 128], lambda k, t0, t1: act_[:, k, t0:t1], 4, [kd, actk], tbs=TBF)
                        for bi, (t0, t1) in enumerate(TBF):
                            S.op("dve", I("tensor_tensor", out=xT[:, d, t0:t1], in0=ps[b_[bi]][:, 0:t1 - t0], in1=xT[:, d, t0:t1], op=ALU.add),
                                 reads=[("ps", b_[bi]), ("xT", d)], writes=[("xT", d)])
            S.barrier()
            AFa.release(mFF)

        chk(10)
        rs, rsk, mk = rmsnorm_to_hT(PV_NFIN, "nf")
        yb = [AFa.alloc([T], "yb%d" % i) for i in range(2)]
        for c in range(16):
            ya, yk = yb[c % 2]
            S.op("dve", I("scalar_tensor_tensor", out=ya, in0=xT[:, c, :], scalar=pv[:, PV_NFIN + c:PV_NFIN + c + 1], in1=rs, op0=ALU.mult, op1=ALU.mult),
                 reads=[("xT", c), rsk, "pv"], writes=[yk])
            S.dma("sp", I("dma_start", out=yT[:, c, :], in_=ya), reads=[yk], writes=[("yT", c)])
        S.emit(block)
    return nc


def _host_layout(inp, core):
    f = np.float32
    b, half = core // 2, core % 2
    s0 = core * NSMP
    xp = inp["x_prompt"][b, half * TP:(half + 1) * TP]
    xsm = inp["x_sample"][s0:s0 + NSMP, 0]
    xtok = np.concatenate([xp, xsm], axis=0)
    xT = np.ascontiguousarray(xtok.reshape(T, 16, 128).transpose(2, 1, 0))
    sca = np.ascontiguousarray(inp["state_conv_a"][:, s0:s0 + NSMP].reshape(2, NSMP, 2, 4, 128).transpose(0, 4, 3, 2, 1))
    sgl = np.ascontiguousarray(inp["state_gla"][:, s0:s0 + NSMP])
    scc = np.ascontiguousarray(inp["state_conv_c"][:, s0:s0 + NSMP].reshape(2, NSMP, 30, 4, 128).transpose(0, 4, 3, 1, 2))
    wsel = np.zeros((128, 8), f)
    if half == 1:
        wsel[:, 0] = 1.0
    return dict(xT_in=xT, sca=sca, sgl=sgl, scc=scc, wsel=wsel)


def _shared_layout(inp):
    f = np.float32
    pv = np.zeros((128, NPV), f)

    def fm(v, nchunk):
        return np.asarray(v, f).reshape(nchunk, 128).T

    for l in range(2):
        P = l * PVL
        pv[:, P + PV_NMIX:P + PV_NMIX + 16] = fm(inp["norm_mix"][l], 16)
        pv[:, P + PV_NFFN:P + PV_NFFN + 16] = fm(inp["norm_ffn"][l], 16)
        pv[:, P + PV_CAW:P + PV_CAW + 12] = np.asarray(inp["conv_a_w"][l], f).reshape(3, 4, 128).transpose(2, 1, 0).reshape(128, 12)
        pv[:, P + PV_GNORM:P + PV_GNORM + 4] = fm(inp["gla_norm"][l], 4)
        pv[:, P + PV_CCW:P + PV_CCW + 124] = np.asarray(inp["conv_c_w"][l], f).reshape(31, 4, 128).transpose(2, 1, 0).reshape(128, 124)
        pv[:, P + PV_CCB:P + PV_CCB + 4] = fm(inp["conv_c_b"][l], 4)
        pv[:, P + PV_LCG:P + PV_LCG + 4] = fm(inp["ln_c_g"][l], 4)
        pv[:, P + PV_LCB:P + PV_LCB + 4] = fm(inp["ln_c_b"][l], 4)
        pv[:, P + PV_W00:P + PV_W00 + 4] = np.broadcast_to(np.asarray(inp["sg_w"][l, :, 0, 0], f)[None, :], (128, 4))
        pv[:, P + PV_B0:P + PV_B0 + 4] = np.broadcast_to(np.asarray(inp["sg_b"][l, :, 0], f)[None, :], (128, 4))
    pv[:, PV_NFIN:PV_NFIN + 16] = fm(inp["norm_final"], 16)
    bc = np.zeros((2, 128, 1792), f)
    for l in range(2):
        row = np.concatenate([inp["ln_d_g"][l], inp["ln_d_b"][l], inp["gla_a_bias"][l], np.asarray(inp["sg_b"][l], f).reshape(-1)])
        bc[l] = np.broadcast_to(np.asarray(row, f)[None, :], (128, 1792))
    sgwT = np.ascontiguousarray(np.asarray(inp["sg_w"], f).transpose(0, 3, 1, 2))
    consts = np.zeros((128, 256), f)
    consts[:, 0:128] = np.eye(128, dtype=f)
    consts[:, 128:256] = np.triu(np.ones((128, 128), f))
    return dict(pvec=pv, bc_in=bc, sgwT=sgwT, a2_in=np.ascontiguousarray(inp["gla_a2"], dtype=f), consts=consts,
                w_in=np.ascontiguousarray(inp["w_in"], dtype=f), w_o=np.ascontiguousarray(inp["w_o"], dtype=f),
                w_gate=np.ascontiguousarray(inp["w_gate"], dtype=f), w_up=np.ascontiguousarray(inp["w_up"], dtype=f),
                w_down=np.ascontiguousarray(inp["w_down"], dtype=f))


def kernel(**inputs):
    inp = {k: np.asarray(v) for k, v in inputs.items()}
    shared = _shared_layout(inp)
    in_maps = []
    for c in range(NCORES):
        m = dict(shared)
        m.update(_host_layout(inp, c))
        in_maps.append(m)
    import os
    nc = build_nc(NCORES, int(os.environ.get('KSTOP_DBG', '0')))
    res = run_bass_kernel_spmd(nc, in_maps, core_ids=list(range(NCORES)))
    R = res.results
    f = np.float32
    y_prompt = np.zeros((4, 2048, D), f)
    y_sample = np.zeros((128, 1, D), f)
    ca_p = np.zeros((2, 4, 2, 512), f); ca_s = np.zeros((2, 128, 2, 512), f)
    gl_p = np.zeros((2, 4, 4, 64, 128), f); gl_s = np.zeros((2, 128, 4, 64, 128), f)
    cc_p = np.zeros((2, 4, 30, 512), f); cc_s = np.zeros((2, 128, 30, 512), f)
    sv_p = np.zeros((2, 4, 128, 512), f); sv_s = np.zeros((2, 128, 1, 512), f)
    for c in range(NCORES):
        r = R[c]
        b, half = c // 2, c % 2
        s0 = c * NSMP
        ytok = np.asarray(r["yT"]).transpose(2, 1, 0).reshape(T, D)
        y_prompt[b, half * TP:(half + 1) * TP] = ytok[:TP]
        y_sample[s0:s0 + NSMP, 0] = ytok[TP:]
        ca_s[:, s0:s0 + NSMP] = np.asarray(r["o_ca_s"]).transpose(0, 4, 3, 2, 1).reshape(2, NSMP, 2, 512)
        gl_s[:, s0:s0 + NSMP] = np.asarray(r["o_gl_s"])
        cc_s[:, s0:s0 + NSMP] = np.asarray(r["o_cc_s"]).transpose(0, 3, 4, 2, 1).reshape(2, NSMP, 30, 512)
        sv_s[:, s0:s0 + NSMP, 0] = np.asarray(r["o_sv_s"])
        if half == 1:
            ca_p[:, b] = np.asarray(r["o_ca_p"]).transpose(0, 3, 2, 1).reshape(2, 2, 512)
            gl_p[:, b] = np.asarray(r["o_gl_p"]).reshape(2, 4, 64, 128)
            cc_p[:, b] = np.asarray(r["o_cc_p"]).transpose(0, 3, 2, 1).reshape(2, 30, 512)
            sv_p[:, b] = np.asarray(r["o_sv_p"])
    return (y_prompt, y_sample, ca_p, ca_s, gl_p, gl_s, cc_p, cc_s, sv_p, sv_s)
```

```python
import numpy as np
from contextlib import ExitStack
import concourse.bass as bass
import concourse.mybir as mybir
from concourse.bass_utils import run_bass_kernel_spmd

F32 = mybir.dt.float32
BF16 = mybir.dt.bfloat16
AF = mybir.ActivationFunctionType
ALU = mybir.AluOpType
AX = mybir.AxisListType

NCORES = 8
D = 2048
TP = 1024
NSMP = 16
T = TP + NSMP
TB = [(0, 512), (512, 1024), (1024, 1040)]
TBF = [(0, 352), (352, 704), (704, 1040)]
DFF = 5632
NIN = 5136
EPS = 1e-6
NSLOT = 3
C_AX, C_AB, C_AC, C_Q, C_K, C_V, C_R, C_LR, C_CA, C_CG, C_DU, C_DV = (
    0, 512, 1024, 1536, 1792, 2048, 2560, 3072, 3088, 3600, 4112, 4624)
PV_NMIX, PV_NFFN = 0, 16
PV_CAW = 32
PV_GNORM = 44
PV_CCW = 48
PV_CCB = 172
PV_LCG = 176
PV_LCB = 180
PV_W00 = 184
PV_B0 = 188
PVL = 192
PV_NFIN = 2 * PVL
NPV = 2 * PVL + 16
XS_S, XS_A, XS_C, XSN = 0, 256, 264, 384


def I(name, *a, **k):
    return lambda e: getattr(e, name)(*a, **k)


class Sched:
    ENGS = ("pe", "act", "dve", "pool", "sp")
    NDMA = 8

    def __init__(self, nc, stack):
        self.nc = nc
        self.ops = []
        self.last_w = {}
        self.readers = {}
        self.sems = {e: stack.enter_context(nc.semaphore("s_" + e)) for e in self.ENGS}
        self.dsems = {q: [stack.enter_context(nc.semaphore("d_%s%d" % (q, i))) for i in range(self.NDMA)]
                      for q in ("sp", "pool")}
        self.dcount = {q: 0 for q in self.dsems}
        self.dlast = {q: [None] * self.NDMA for q in self.dsems}
        self.custom = []
        self.dead = False

    def _deps(self, reads, writes):
        deps = set()
        for k in reads:
            w = self.last_w.get(k)
            if w is not None:
                deps.add(w)
        for k in writes:
            w = self.last_w.get(k)
            if w is not None:
                deps.add(w)
            deps.update(self.readers.get(k, ()))
        return deps

    def _commit(self, oid, reads, writes):
        for k in reads:
            self.readers.setdefault(k, []).append(oid)
        for k in writes:
            self.last_w[k] = oid
            self.readers[k] = []

    def op(self, eng, fn, reads=(), writes=(), tok=None):
        if self.dead:
            return None
        psr = [k for k in reads if isinstance(k, tuple) and k[0] == "ps"]
        if psr:
            reads = [k for k in reads if k not in psr]
            writes = list(writes) + psr
        deps = self._deps(reads, writes)
        oid = len(self.ops)
        self.ops.append(dict(eng=eng, fn=fn, deps=deps, dma=None, tok=tok))
        self._commit(oid, reads, writes)
        if tok is not None:
            self.custom.append(oid)
        return oid

    def dma(self, q, fn, reads=(), writes=()):
        if self.dead:
            return None
        deps = self._deps(reads, writes)
        i = self.dcount[q]
        self.dcount[q] += 1
        slot = i % self.NDMA
        val = 16 * (i // self.NDMA + 1)
        prev = self.dlast[q][slot]
        if prev is not None:
            deps.add(prev)
        oid = len(self.ops)
        self.dlast[q][slot] = oid
        self.ops.append(dict(eng=q, fn=fn, deps=deps, dma=(self.dsems[q][slot], val), tok=None))
        self._commit(oid, reads, writes)
        return oid

    def barrier(self):
        if self.dead:
            return
        engs = ("pe", "act", "dve", "sp")
        last = {}
        for oid in range(len(self.ops) - 1, -1, -1):
            o = self.ops[oid]
            if o["dma"] is None and o["tok"] is None and o["eng"] in engs and o["eng"] not in last and o["fn"] is not None:
                last[o["eng"]] = oid
            if len(last) == 3:
                break
        dm = [x for x in self.dlast["sp"] if x is not None]
        for e in engs:
            deps = set(v for k, v in last.items() if k != e)
            deps.update(dm)
            self.ops.append(dict(eng=e, fn=None, deps=deps, dma=None, tok=None))

    def emit(self, block):
        ops = self.ops
        needed = [False] * len(ops)
        for o in ops:
            for d in o["deps"]:
                needed[d] = True
        tok = [None] * len(ops)
        kind = [None] * len(ops)
        cnt = {e: 0 for e in self.ENGS}
        for i, o in enumerate(ops):
            if o["dma"] is not None:
                tok[i] = o["dma"]
                kind[i] = 16
            elif o["tok"] is not None:
                tok[i] = o["tok"]
                kind[i] = 0
            elif needed[i] and o["fn"] is not None:
                cnt[o["eng"]] += 1
                tok[i] = (self.sems[o["eng"]], cnt[o["eng"]])
                kind[i] = 1
        per = {e: [] for e in self.ENGS}
        seen = {e: {} for e in self.ENGS}
        for i, o in enumerate(ops):
            e = o["eng"]
            waits = []
            for d in sorted(o["deps"]):
                od = ops[d]
                if od["dma"] is None and od["tok"] is None and od["eng"] == e and e == "pe":
                    continue
                t = tok[d]
                if t is None:
                    continue
                sem, val = t
                if seen[e].get(id(sem), 0) >= val:
                    continue
                seen[e][id(sem)] = val
                waits.append((sem, val))
            per[e].append((waits, o["fn"], tok[i], kind[i]))
        finals = []
        for q in self.dlast:
            for oid in self.dlast[q]:
                if oid is not None:
                    finals.append(tok[oid])
        for e in self.ENGS:
            if e != "sp" and cnt[e] > 0:
                finals.append((self.sems[e], cnt[e]))

        def run(engname, eng):
            for waits, fn, t, kd in per[engname]:
                for sem, val in waits:
                    eng.wait_ge(sem, val)
                if fn is None:
                    continue
                ins = fn(eng)
                if t is not None:
                    if kd == 0:
                        ins.then_inc(t[0])
                    else:
                        ins.then_inc(t[0], kd)
            if engname == "sp":
                for sem, val in finals:
                    eng.wait_ge(sem, val)

        block.tensor(lambda e: run("pe", e))
        block.scalar(lambda e: run("act", e))
        block.vector(lambda e: run("dve", e))
        block.gpsimd(lambda e: run("pool", e))
        block.sync(lambda e: run("sp", e))


class Arena:
    def __init__(self, t, n):
        self.t, self.n, self.off, self.cnt = t, n, 0, 0

    def alloc(self, free_shape, name):
        n = int(np.prod(free_shape))
        n = (n + 7) // 8 * 8
        assert self.off + n <= self.n, ("arena overflow", name, self.off, n, self.n)
        ap = self.t[:, self.off:self.off + int(np.prod(free_shape))]
        key = ("ar", id(self), self.off)
        self.off += n
        self.cnt += 1
        if len(free_shape) == 2:
            ap = ap.rearrange("p (a b) -> p a b", a=free_shape[0])
        elif len(free_shape) == 3:
            ap = ap.rearrange("p (a b c) -> p a b c", a=free_shape[0], b=free_shape[1])
        return ap, key

    def mark(self):
        return self.off

    def release(self, m):
        self.off = m


class _Stop(Exception):
    pass


def build_nc(ncores=NCORES, kstop=0):
    nc = bass.Bass("TRN2", target_bir_lowering=False)

    def din(name, shape):
        return nc.dram_tensor(name, list(shape), F32, kind="ExternalInput").ap()

    def dout(name, shape):
        return nc.dram_tensor(name, list(shape), F32, kind="ExternalOutput").ap()

    xT_in = din("xT_in", [128, 16, T])
    sca = din("sca", [2, 128, 4, 2, NSMP])
    sgl = din("sgl", [2, NSMP, 4, 64, 128])
    scc = din("scc", [2, 128, 4, NSMP, 30])
    pvec = din("pvec", [128, NPV])
    bc_in = din("bc_in", [2, 128, 1280 + 512])
    sgwT = din("sgwT", [2, 128, 4, 128])
    a2_in = din("a2_in", [2, 16, 256])
    consts = din("consts", [128, 256])
    wsel_in = din("wsel", [128, 8])
    w_in = din("w_in", [2, D, NIN])
    w_o = din("w_o", [2, D, D])
    w_gate = din("w_gate", [2, D, DFF])
    w_up = din("w_up", [2, D, DFF])
    w_down = din("w_down", [2, DFF, D])

    yT = dout("yT", [128, 16, T])
    o_ca_p = dout("o_ca_p", [2, 128, 4, 2])
    o_ca_s = dout("o_ca_s", [2, 128, 4, 2, NSMP])
    o_gl_p = dout("o_gl_p", [2, 256, 128])
    o_gl_s = dout("o_gl_s", [2, NSMP, 4, 64, 128])
    o_cc_p = dout("o_cc_p", [2, 128, 4, 30])
    o_cc_s = dout("o_cc_s", [2, 128, 4, NSMP, 30])
    o_sv_p = dout("o_sv_p", [2, 128, 512])
    o_sv_s = dout("o_sv_s", [2, NSMP, 512])
    cin = [nc.dram_tensor("cin%d" % l, [128, XSN], F32) for l in range(2)]
    cout = [nc.dram_tensor("cout%d" % l, [2 * 128, XSN], F32) for l in range(2)]

    with ExitStack() as st:
        S = Sched(nc, st)
        ccsem = [st.enter_context(nc.semaphore("cc%d" % l)) for l in range(2)]

        def sb(name, shape, dt):
            return st.enter_context(nc.sbuf_tensor(name, list(shape), dt))

        xT = sb("xT", [128, 16, T], F32)
        hT = sb("hT", [128, 16, T], BF16)
        mixT = sb("mixT", [128, 16, T], BF16)
        ring = [sb("ring%d" % i, [128, 4096], BF16) for i in range(NSLOT)]
        pv = sb("pv", [128, NPV], F32)
        cst = sb("cst", [128, 256], F32)
        cstb = sb("cstb", [128, 256], BF16)
        onesB = sb("onesB", [128, 128], BF16)
        wsel = sb("wselt", [128, 8], F32)
        bct = sb("bct", [128, 1280 + 512], F32)
        wsT = sb("wsT", [128, 4, 128], BF16)
        a2t = sb("a2t", [16, 256], F32)
        xs = sb("xs", [128, XSN], F32)
        xsel = sb("xsel", [128, XSN], F32)
        NAF, NAB = 5300, 8900
        af_t = sb("arenaF", [128, NAF], F32)
        ab_t = sb("arenaB", [128, NAB], BF16)
        AFa, ABa = Arena(af_t, NAF), Arena(ab_t, NAB)
        ps = [st.enter_context(nc.psum_tensor("ps%d" % i, [128, 512], F32)) for i in range(8)]
        block = st.enter_context(nc.Block())
        identF = cst[:, 0:128]
        triuF = cst[:, 128:256]
        identB = cstb[:, 0:128]
        triuB = cstb[:, 128:256]

        pbc = [0]

        reserved = set()

        def newps():
            while True:
                b = pbc[0] % 8
                pbc[0] += 1
                if b not in reserved:
                    return b

        wcount = [0]

        def loadw(src, nk, ncols):
            i = wcount[0] % NSLOT
            wcount[0] += 1
            view = ring[i][:, 0:nk * ncols].rearrange("p (k n) -> p k n", k=nk)
            S.dma("pool", I("dma_start", out=view, in_=src.rearrange("(k p) n -> p k n", p=128)),
                  writes=[("ring", i)])
            return view, ("ring", i)

        def mm_group(lhsT_k, rhs_k, nk, rkeys, tbs=TB, m=128):
            banks = [newps() for _ in tbs]
            for k in range(nk):
                for bi, (t0, t1) in enumerate(tbs):
                    S.op("pe", I("matmul", ps[banks[bi]][0:m, 0:t1 - t0], lhsT=lhsT_k(k), rhs=rhs_k(k, t0, t1),
                                 start=(k == 0), stop=(k == nk - 1)),
                         reads=rkeys, writes=[("ps", banks[bi])])
            return banks

        hkeys = [("hT", c) for c in range(16)]
        mkeys = [("mix", c) for c in range(16)]

        S.dma("sp", I("dma_start", out=pv[:], in_=pvec), writes=["pv"])
        S.dma("sp", I("dma_start", out=cst[:], in_=consts), writes=["cst"])
        S.dma("sp", I("dma_start", out=wsel[:], in_=wsel_in), writes=["wsel"])
        for c in range(16):
            S.dma("sp", I("dma_start", out=xT[:, c, :], in_=xT_in[:, c, :]), writes=[("xT", c)])
        S.op("dve", I("tensor_copy", out=cstb[:], in_=cst[:]), reads=["cst"], writes=["cstb"])
        S.op("dve", I("memset", onesB[:], 1.0), writes=["ones"])

        def rmsnorm_to_hT(gcol0, lname):
            mF, mB = AFa.mark(), ABa.mark()
            sq = [ABa.alloc([T], "sq%d" % i) for i in range(2)]
            rs, rsk = AFa.alloc([T], "rs")
            banks = [newps() for _ in TB]
            for c in range(16):
                sqa, sqk = sq[c % 2]
                if c % 2 == 0:
                    S.op("act", I("activation", out=sqa, in_=xT[:, c, :], func=AF.Square), reads=[("xT", c)], writes=[sqk])
                else:
                    S.op("dve", I("tensor_tensor", out=sqa, in0=xT[:, c, :], in1=xT[:, c, :], op=ALU.mult), reads=[("xT", c)], writes=[sqk])
                for bi, (t0, t1) in enumerate(TB):
                    S.op("pe", I("matmul", ps[banks[bi]][:, 0:t1 - t0], lhsT=onesB[:], rhs=sqa[:, t0:t1],
                                 start=(c == 0), stop=(c == 15)), reads=[sqk, "ones"], writes=[("ps", banks[bi])])
            for bi, (t0, t1) in enumerate(TB):
                S.op("act", I("activation", out=rs[:, t0:t1], in_=ps[banks[bi]][:, 0:t1 - t0], func=AF.Sqrt,
                              bias=EPS, scale=1.0 / D), reads=[("ps", banks[bi])], writes=[rsk])
            S.op("dve", I("reciprocal", out=rs, in_=rs), reads=[rsk], writes=[rsk])
            return rs, rsk, (mF, mB)

        def norm_apply(rs, rsk, gcol0):
            for c in range(16):
                S.op("dve", I("scalar_tensor_tensor", out=hT[:, c, :], in0=xT[:, c, :], scalar=pv[:, gcol0 + c:gcol0 + c + 1],
                              in1=rs, op0=ALU.mult, op1=ALU.mult), reads=[("xT", c), rsk, "pv"], writes=[("hT", c)])

        def win_cols(l, c0, n):
            return w_in[l, :, c0:c0 + n]

        def chk(n):
            if kstop and n == kstop:
                S.dead = True

        for l in range(2):
            P = l * PVL
            S.dma("sp", I("dma_start", out=bct[:], in_=bc_in[l]), writes=["bct"])
            S.dma("sp", I("dma_start", out=a2t[:], in_=a2_in[l]), writes=["a2t"])
            rs, rsk, mk = rmsnorm_to_hT(P + PV_NMIX, "n1")
            norm_apply(rs, rsk, P + PV_NMIX)
            S.barrier()
            AFa.release(mk[0]); ABa.release(mk[1])
            mL_F, mL_B = AFa.mark(), ABa.mark()
            headA, hak = AFa.alloc([4, 4], "headA")
            extC, eck = ABa.alloc([4, 30 + T], "extC")
            mP_F, mP_B = AFa.mark(), ABa.mark()
            wtmp, wtk = AFa.alloc([4, 128], "wtmp")
            S.dma("sp", I("dma_start", out=wtmp, in_=sgwT[l]), writes=[wtk])
            S.op("dve", I("tensor_tensor", out=wsT[:], in0=wtmp, in1=triuF.unsqueeze(1).to_broadcast([128, 4, 128]),
                          op=ALU.mult), reads=[wtk, "cst"], writes=["wsT"])

            chk(1)
            hisA, hisk = AFa.alloc([4, 2, NSMP], "hisA")
            oas, oask = AFa.alloc([4, 2, NSMP], "oas")
            S.dma("sp", I("dma_start", out=hisA, in_=sca[l]), writes=[hisk])
            tacc = [AFa.alloc([T], "tacc%d" % i) for i in range(2)]
            exts = [AFa.alloc([T + 2], "extA%d" % i) for i in range(2)]
            for j in range(2):
                S.op("dve", I("memset", exts[j][0][:, 0:2], 0.0), writes=[exts[j][1]])
            for hb in range(2):
                sc, kc = loadw(win_cols(l, C_AC + hb * 256, 256), 16, 256)
                for j in range(2):
                    tmpc, tck = tacc[j]
                    bc_ = mm_group(lambda k, s=sc, j=j: s[:, k, j * 128:(j + 1) * 128], lambda k, t0, t1: hT[:, k, t0:t1], 16, [kc] + hkeys)
                    for bi, (t0, t1) in enumerate(TB):
                        S.op("act", I("copy", out=tmpc[:, t0:t1], in_=ps[bc_[bi]][:, 0:t1 - t0]), reads=[("ps", bc_[bi])], writes=[tck])
                sx, kx = loadw(win_cols(l, C_AX + hb * 256, 256), 16, 256)
                for j in range(2):
                    cc = hb * 2 + j
                    tmpc, tck = tacc[j]
                    acc, ack = tacc[j]
                    ext, exk = exts[j]
                    w0 = pv[:, P + PV_CAW + cc * 3 + 0:P + PV_CAW + cc * 3 + 1]
                    w1 = pv[:, P + PV_CAW + cc * 3 + 1:P + PV_CAW + cc * 3 + 2]
                    w2 = pv[:, P + PV_CAW + cc * 3 + 2:P + PV_CAW + cc * 3 + 3]
                    bx = mm_group(lambda k, s=sx, j=j: s[:, k, j * 128:(j + 1) * 128], lambda k, t0, t1: hT[:, k, t0:t1], 16, [kx] + hkeys)
                    for bi, (t0, t1) in enumerate(TB):
                        S.op("dve", I("tensor_tensor", out=ext[:, 2 + t0:2 + t1], in0=ps[bx[bi]][:, 0:t1 - t0], in1=tmpc[:, t0:t1], op=ALU.mult),
                             reads=[("ps", bx[bi]), tck], writes=[exk])
                    S.op("dve", I("tensor_scalar", out=acc[:, 0:TP], in0=ext[:, 2:2 + TP], scalar1=w2, scalar2=None, op0=ALU.mult),
                         reads=[exk, "pv"], writes=[ack])
                    S.op("dve", I("scalar_tensor_tensor", out=acc[:, 0:TP], in0=ext[:, 1:1 + TP], scalar=w1, in1=acc[:, 0:TP], op0=ALU.mult, op1=ALU.add),
                         reads=[exk, ack], writes=[ack])
                    S.op("dve", I("scalar_tensor_tensor", out=acc[:, 0:TP], in0=ext[:, 0:TP], scalar=w0, in1=acc[:, 0:TP], op0=ALU.mult, op1=ALU.add),
                         reads=[exk, ack], writes=[ack])
                    S.op("dve", I("tensor_scalar", out=acc[:, TP:T], in0=ext[:, 2 + TP:2 + T], scalar1=w2, scalar2=None, op0=ALU.mult),
                         reads=[exk], writes=[ack])
                    S.op("dve", I("scalar_tensor_tensor", out=acc[:, TP:T], in0=hisA[:, cc, 1, :], scalar=w1, in1=acc[:, TP:T], op0=ALU.mult, op1=ALU.add),
                         reads=[hisk, ack], writes=[ack])
                    S.op("dve", I("scalar_tensor_tensor", out=acc[:, TP:T], in0=hisA[:, cc, 0, :], scalar=w0, in1=acc[:, TP:T], op0=ALU.mult, op1=ALU.add),
                         reads=[hisk, ack], writes=[ack])
                    S.op("dve", I("tensor_copy", out=xs[:, XS_A + cc * 2:XS_A + cc * 2 + 2], in_=ext[:, TP:TP + 2]), reads=[exk], writes=["xs"])
                    S.op("dve", I("tensor_copy", out=oas[:, cc, 1, :], in_=ext[:, 2 + TP:2 + T]), reads=[exk], writes=[oask])
                    S.op("dve", I("tensor_copy", out=oas[:, cc, 0, :], in_=hisA[:, cc, 1, :]), reads=[hisk], writes=[oask])
                    S.op("dve", I("tensor_copy", out=headA[:, cc, 0:2], in_=acc[:, 0:2]), reads=[ack], writes=[hak])
                sbb, kb = loadw(win_cols(l, C_AB + hb * 256, 256), 16, 256)
                for j in range(2):
                    cc = hb * 2 + j
                    acc, ack = tacc[j]
                    bb = mm_group(lambda k, s=sbb, j=j: s[:, k, j * 128:(j + 1) * 128], lambda k, t0, t1: hT[:, k, t0:t1], 16, [kb] + hkeys)
                    S.op("dve", I("tensor_copy", out=headA[:, cc, 2:4], in_=ps[bb[0]][:, 0:2]), reads=[("ps", bb[0])], writes=[hak])
                    for bi, (t0, t1) in enumerate(TB):
                        S.op("dve", I("tensor_tensor", out=mixT[:, cc, t0:t1], in0=ps[bb[bi]][:, 0:t1 - t0], in1=acc[:, t0:t1], op=ALU.mult),
                             reads=[("ps", bb[bi]), ack], writes=[("mix", cc)])
            S.dma("sp", I("dma_start", out=o_ca_s[l], in_=oas), reads=[oask], writes=["o_ca_s"])
            S.dma("sp", I("dma_start", out=o_ca_p[l], in_=xs[:, XS_A:XS_A + 8].rearrange("p (c r) -> p c r", c=4)), reads=["xs"], writes=["o_ca_p"])
            S.barrier()
            AFa.release(mP_F); ABa.release(mP_B)

            chk(2)
            ocs, ocsk = AFa.alloc([4, NSMP, 30], "ocs")
            hisC, hck = AFa.alloc([4, NSMP, 30], "hisC")
            S.dma("sp", I("dma_start", out=hisC, in_=scc[l]), writes=[hck])
            tmpg, tgk = AFa.alloc([T], "tmpg")
            for hb in range(2):
                sa, ka = loadw(win_cols(l, C_CA + hb * 256, 256), 16, 256)
                sg_, kg = loadw(win_cols(l, C_CG + hb * 256, 256), 16, 256)
                for j in range(2):
                    cc = hb * 2 + j
                    bg = mm_group(lambda k, s=sg_, j=j: s[:, k, j * 128:(j + 1) * 128], lambda k, t0, t1: hT[:, k, t0:t1], 16, [kg] + hkeys)
                    for bi, (t0, t1) in enumerate(TB):
                        S.op("act", I("activation", out=tmpg[:, t0:t1], in_=ps[bg[bi]][:, 0:t1 - t0], func=AF.Sigmoid), reads=[("ps", bg[bi])], writes=[tgk])
                    ba = mm_group(lambda k, s=sa, j=j: s[:, k, j * 128:(j + 1) * 128], lambda k, t0, t1: hT[:, k, t0:t1], 16, [ka] + hkeys)
                    for bi, (t0, t1) in enumerate(TB):
                        S.op("dve", I("tensor_tensor", out=extC[:, cc, 30 + t0:30 + t1], in0=ps[ba[bi]][:, 0:t1 - t0], in1=tmpg[:, t0:t1], op=ALU.mult),
                             reads=[("ps", ba[bi]), tgk], writes=[eck])
                    S.op("dve", I("tensor_tensor", out=xs[:, XS_C + cc * 30:XS_C + cc * 30 + 30], in0=ps[ba[1]][:, 482:512], in1=tmpg[:, TP - 30:TP], op=ALU.mult),
                         reads=[("ps", ba[1]), tgk], writes=["xs"])
                    S.op("dve", I("tensor_tensor", out=ocs[:, cc, :, 29], in0=ps[ba[2]][:, 0:NSMP], in1=tmpg[:, TP:T], op=ALU.mult),
                         reads=[("ps", ba[2]), tgk], writes=[ocsk])
                    S.op("dve", I("tensor_copy", out=ocs[:, cc, :, 0:29], in_=hisC[:, cc, :, 1:30]), reads=[hck], writes=[ocsk])
            S.dma("sp", I("dma_start", out=o_cc_s[l], in_=ocs), reads=[ocsk], writes=["o_cc_s"])
            S.dma("sp", I("dma_start", out=o_cc_p[l], in_=xs[:, XS_C:XS_C + 120].rearrange("p (c r) -> p c r", c=4)), reads=["xs"], writes=["o_cc_p"])
            S.barrier()
            AFa.release(mP_F); ABa.release(mP_B)

            chk(3)
            o_bf = mixT[:, 4:8, :]
            obk = [("mix", 4 + h) for h in range(4)]
            qT = mixT[:, 8:10, :]
            kT = mixT[:, 10:12, :]
            qk = [("mix", 8), ("mix", 9)]
            kk = [("mix", 10), ("mix", 11)]
            qh32, qhk = AFa.alloc([TP], "qhat32")
            qhat = qh32.bitcast(BF16).rearrange("p (a b) -> p a b", a=2)
            ePc, epk = AFa.alloc([2], "ePc")
            Sst, ssk = AFa.alloc([2, 128], "Sst")
            mG_F, mG_B = AFa.mark(), ABa.mark()
            glr, glk = AFa.alloc([T], "glr")
            Sbf, sbk = ABa.alloc([4, 128], "Sbm")
            S.op("dve", I("memset", ePc, 1.0), writes=[epk])
            S.op("dve", I("memset", Sst, 0.0), writes=[ssk])
            S.op("dve", I("memset", Sbf, 0.0), writes=[sbk])
            sq_, kq_ = loadw(win_cols(l, C_Q, 256), 16, 256)
            for hh in range(2):
                b_ = mm_group(lambda k, hh=hh: sq_[:, k, hh * 128:(hh + 1) * 128], lambda k, t0, t1: hT[:, k, t0:t1], 16, [kq_] + hkeys)
                for bi, (t0, t1) in enumerate(TB):
                    S.op("act", I("copy", out=qT[:, hh, t0:t1], in_=ps[b_[bi]][:, 0:t1 - t0]), reads=[("ps", b_[bi])], writes=[qk[hh]])
            sk_, kk_ = loadw(win_cols(l, C_K, 256), 16, 256)
            for hh in range(2):
                b_ = mm_group(lambda k, hh=hh: sk_[:, k, hh * 128:(hh + 1) * 128], lambda k, t0, t1: hT[:, k, t0:t1], 16, [kk_] + hkeys)
                for bi, (t0, t1) in enumerate(TB):
                    S.op("act", I("copy", out=kT[:, hh, t0:t1], in_=ps[b_[bi]][:, 0:t1 - t0]), reads=[("ps", b_[bi])], writes=[kk[hh]])
            slr, klr = loadw(win_cols(l, C_LR, 16), 16, 16)
            b_ = mm_group(lambda k: slr[:, k, 0:16], lambda k, t0, t1: hT[:, k, t0:t1], 16, [klr] + hkeys, m=16)
            for bi, (t0, t1) in enumerate(TB):
                S.op("act", I("copy", out=glr[0:16, t0:t1], in_=ps[b_[bi]][0:16, 0:t1 - t0]), reads=[("ps", b_[bi])], writes=[glk])
            kq, kqk = AFa.alloc([512], "kq")
            bkq = newps()
            for which, (sw, kw) in enumerate(((sk_, kk_), (sq_, kq_))):
                for k in range(16):
                    S.op("pe", I("matmul", ps[bkq][0:NSMP, which * 256:(which + 1) * 256], lhsT=hT[:, k, TP:T], rhs=sw[:, k, :],
                                 start=(k == 0), stop=(k == 15)), reads=[kw, ("hT", k)], writes=[("ps", bkq)])
            S.op("act", I("copy", out=kq[0:NSMP, :], in_=ps[bkq][0:NSMP, :]), reads=[("ps", bkq)], writes=[kqk])
            chk(31)
            sv0, kv0 = loadw(win_cols(l, C_V, 256), 16, 256)
            sv1, kv1 = loadw(win_cols(l, C_V + 256, 256), 16, 256)
            mTF, mTB = AFa.mark(), ABa.mark()
            frames = []
            for p_ in range(2):
                fr = dict(sp=AFa.alloc([256], "sp"), eb=AFa.alloc([2, 128], "eb"), enb=AFa.alloc([2, 128], "enb"), ebl=AFa.alloc([2], "ebl"),
                          khT=AFa.alloc([2, 128], "khT"), vbf=ABa.alloc([512], "vbf"), qt=ABa.alloc([2, 128], "qt"), khtok=ABa.alloc([256], "khtok"),
                          att=ABa.alloc([4, 128], "att"), kt=ABa.alloc([4, 128], "ktm"))
                S.op("dve", I("memset", fr["kt"][0], 0.0), writes=[fr["kt"][1]])
                frames.append(fr)

            def g_stage_a(n):
                fr = frames[n % 2]
                c0 = n * 128
                sp_, spk = fr["sp"]; eb, ebk = fr["eb"]; enb, enk = fr["enb"]; ebl, eblk = fr["ebl"]; khT, khk = fr["khT"]
                vbf, vbk = fr["vbf"]; qt, qtk = fr["qt"]; khtok, khtk = fr["khtok"]; att, atk = fr["att"]; kt, ktk = fr["kt"]
                bl = newps()
                S.op("pe", I("matmul", ps[bl][:, 0:256], lhsT=glr[0:16, c0:c0 + 128], rhs=a2t[:, :], start=True, stop=True),
                     reads=[glk, "a2t"], writes=[("ps", bl)])
                S.op("dve", I("tensor_tensor", out=sp_, in0=ps[bl][:, 0:256], in1=bct[:, 1024:1280], op=ALU.add),
                     reads=[("ps", bl), "bct"], writes=[spk])
                S.op("act", I("activation", out=sp_, in_=sp_, func=AF.Exp, scale=-1.0), reads=[spk], writes=[spk])
                S.op("act", I("activation", out=sp_, in_=sp_, func=AF.Ln, bias=1.0), reads=[spk], writes=[spk])
                bv = newps()
                for half, (sv, kv) in enumerate(((sv0, kv0), (sv1, kv1))):
                    for k in range(16):
                        S.op("pe", I("matmul", ps[bv][:, half * 256:(half + 1) * 256], lhsT=hT[:, k, c0:c0 + 128], rhs=sv[:, k, :],
                                     start=(k == 0), stop=(k == 15)), reads=[kv, ("hT", k)], writes=[("ps", bv)])
                S.op("act", I("copy", out=vbf, in_=ps[bv][:, :]), reads=[("ps", bv)], writes=[vbk])
                bcs = newps()
                for hh in range(2):
                    S.op("pe", I("matmul", ps[bcs][:, hh * 128:(hh + 1) * 128], lhsT=sp_[:, hh * 128:(hh + 1) * 128], rhs=triuF,
                                 start=True, stop=True), reads=[spk, "cst"], writes=[("ps", bcs)])
                S.op("act", I("activation", out=eb, in_=ps[bcs][:, 0:256].rearrange("p (a b) -> p a b", a=2), func=AF.Exp, scale=-1.0 / 16), reads=[("ps", bcs)], writes=[ebk])
                S.op("act", I("activation", out=enb, in_=ps[bcs][:, 0:256].rearrange("p (a b) -> p a b", a=2), func=AF.Exp, scale=1.0 / 16), reads=[("ps", bcs)], writes=[enk])
                S.op("dve", I("tensor_copy", out=ebl, in_=eb[:, :, 127]), reads=[ebk], writes=[eblk])
                for hh in range(2):
                    S.op("dve", I("scalar_tensor_tensor", out=qt[:, hh, :], in0=qT[:, hh, c0:c0 + 128], scalar=0.125, in1=eb[:, hh, :], op0=ALU.mult, op1=ALU.mult),
                         reads=[qk[hh], ebk], writes=[qtk])
                    S.op("dve", I("scalar_tensor_tensor", out=khT[:, hh, :], in0=kT[:, hh, c0:c0 + 128], scalar=ebl[:, hh:hh + 1], in1=enb[:, hh, :], op0=ALU.mult, op1=ALU.mult),
                         reads=[kk[hh], enk, eblk], writes=[khk])
                    S.op("dve", I("tensor_scalar", out=qhat[:, hh, c0:c0 + 128], in0=qt[:, hh, :], scalar1=ePc[:, hh:hh + 1], scalar2=None, op0=ALU.mult),
                         reads=[qtk, epk], writes=[qhk])
                for hp in range(2):
                    r0 = hp * 64
                    S.op("dve", I("tensor_tensor", out=kt[r0:r0 + 64, hp:4:2, :], in0=kT[r0:r0 + 64, :, c0:c0 + 128], in1=enb[r0:r0 + 64, :, :], op=ALU.mult),
                         reads=kk + [enk], writes=[ktk])
                S.op("dve", I("tensor_tensor", out=ePc, in0=ePc, in1=ebl, op=ALU.mult), reads=[epk, eblk, qhk], writes=[epk])
                btr = newps()
                for hh in range(2):
                    S.op("pe", I("transpose", ps[btr][:, hh * 128:(hh + 1) * 128], khT[:, hh, :], identF), reads=[khk, "cst"], writes=[("ps", btr)])
                S.op("act", I("copy", out=khtok, in_=ps[btr][:, 0:256]), reads=[("ps", btr)], writes=[khtk])
                batt = newps()
                for h in range(4):
                    S.op("pe", I("matmul", ps[batt][:, h * 128:(h + 1) * 128], lhsT=kt[:, h, :], rhs=qt[:, h // 2, :], start=True, stop=True),
                         reads=[ktk, qtk], writes=[("ps", batt)])
                S.op("dve", I("tensor_tensor", out=att, in0=ps[batt][:, :].rearrange("p (h i) -> p h i", h=4), in1=triuF.unsqueeze(1).to_broadcast([128, 4, 128]), op=ALU.mult),
                     reads=[("ps", batt), "cst"], writes=[atk])

            def g_stage_b(n):
                fr = frames[n % 2]
                c0 = n * 128
                ebl, eblk = fr["ebl"]; vbf, vbk = fr["vbf"]; qt, qtk = fr["qt"]; khtok, khtk = fr["khtok"]; att, atk = fr["att"]
                bo = newps()
                for h in range(4):
                    hh = h // 2
                    S.op("pe", I("matmul", ps[bo][:, h * 128:(h + 1) * 128], lhsT=vbf[:, h * 128:(h + 1) * 128], rhs=att[:, h, :], start=True, stop=False),
                         reads=[vbk, atk], writes=[("ps", bo)])
                    S.op("pe", I("matmul", ps[bo][:, h * 128:(h + 1) * 128], lhsT=Sbf[:, h, :], rhs=qt[:, hh, :], start=False, stop=True),
                         reads=[sbk, qtk], writes=[("ps", bo)])
                S.op("act", I("copy", out=o_bf[:, :, c0:c0 + 128], in_=ps[bo][:, :].rearrange("p (h i) -> p h i", h=4)), reads=[("ps", bo)], writes=obk)
                bs = newps()
                for h in range(4):
                    hh = h // 2
                    S.op("pe", I("matmul", ps[bs][:, h * 128:(h + 1) * 128], lhsT=khtok[:, hh * 128:(hh + 1) * 128], rhs=vbf[:, h * 128:(h + 1) * 128], start=True, stop=True),
                         reads=[khtk, vbk], writes=[("ps", bs)])
                for h in range(4):
                    p0, hh = (h % 2) * 64, h // 2
                    S.op("dve", I("scalar_tensor_tensor", out=Sst[p0:p0 + 64, hh, :], in0=Sst[p0:p0 + 64, hh, :], scalar=ebl[p0:p0 + 64, hh:hh + 1],
                                  in1=ps[bs][p0:p0 + 64, h * 128:(h + 1) * 128], op0=ALU.mult, op1=ALU.add),
                         reads=[ssk, eblk, ("ps", bs)], writes=[ssk])
                for hp in range(2):
                    r0 = hp * 64
                    S.op("act", I("copy", out=Sbf[r0:r0 + 64, hp:4:2, :], in_=Sst[r0:r0 + 64, :, :]), reads=[ssk], writes=[sbk])

            g_stage_a(0)
            for n in range(8):
                if n + 1 < 8:
                    g_stage_a(n + 1)
                g_stage_b(n)
            for n in (8,):
                if n == 8:
                    chk(32)
                    S.barrier()
                AFa.release(mTF); ABa.release(mTB)
                c0 = n * 128
                m = 128 if n < 8 else NSMP
                sp_, spk = AFa.alloc([256], "sp")
                bl = newps()
                S.op("pe", I("matmul", ps[bl][0:m, 0:256], lhsT=glr[0:16, c0:c0 + m], rhs=a2t[:, :], start=True, stop=True),
                     reads=[glk, "a2t"], writes=[("ps", bl)])
                S.op("dve", I("tensor_tensor", out=sp_[0:m, :], in0=ps[bl][0:m, 0:256], in1=bct[0:m, 1024:1280], op=ALU.add),
                     reads=[("ps", bl), "bct"], writes=[spk])
                S.op("act", I("activation", out=sp_[0:m, :], in_=sp_[0:m, :], func=AF.Exp, scale=-1.0), reads=[spk], writes=[spk])
                S.op("act", I("activation", out=sp_[0:m, :], in_=sp_[0:m, :], func=AF.Ln, bias=1.0), reads=[spk], writes=[spk])
                bv = newps()
                for half, (sv, kv) in enumerate(((sv0, kv0), (sv1, kv1))):
                    for k in range(16):
                        S.op("pe", I("matmul", ps[bv][0:m, half * 256:(half + 1) * 256], lhsT=hT[:, k, c0:c0 + m], rhs=sv[:, k, :],
                                     start=(k == 0), stop=(k == 15)), reads=[kv, ("hT", k)], writes=[("ps", bv)])
                if n < 8:
                    vbf, vbk = ABa.alloc([512], "vbf")
                    S.op("act", I("copy", out=vbf, in_=ps[bv][:, :]), reads=[("ps", bv)], writes=[vbk])
                    bcs = newps()
                    for hh in range(2):
                        S.op("pe", I("matmul", ps[bcs][:, hh * 128:(hh + 1) * 128], lhsT=sp_[:, hh * 128:(hh + 1) * 128], rhs=triuF,
                                     start=True, stop=True), reads=[spk, "cst"], writes=[("ps", bcs)])
                    eb, ebk = AFa.alloc([2, 128], "eb")
                    enb, enk = AFa.alloc([2, 128], "enb")
                    ebl, eblk = AFa.alloc([2], "ebl")
                    S.op("act", I("activation", out=eb, in_=ps[bcs][:, 0:256].rearrange("p (a b) -> p a b", a=2), func=AF.Exp, scale=-1.0 / 16), reads=[("ps", bcs)], writes=[ebk])
                    S.op("act", I("activation", out=enb, in_=ps[bcs][:, 0:256].rearrange("p (a b) -> p a b", a=2), func=AF.Exp, scale=1.0 / 16), reads=[("ps", bcs)], writes=[enk])
                    S.op("dve", I("tensor_copy", out=ebl, in_=eb[:, :, 127]), reads=[ebk], writes=[eblk])
                    qt, qtk = ABa.alloc([2, 128], "qt")
                    khT, khk = AFa.alloc([2, 128], "khT")
                    khtok, khtk = ABa.alloc([256], "khtok")
                    for hh in range(2):
                        S.op("dve", I("scalar_tensor_tensor", out=qt[:, hh, :], in0=qT[:, hh, c0:c0 + 128], scalar=0.125, in1=eb[:, hh, :], op0=ALU.mult, op1=ALU.mult),
                             reads=[qk[hh], ebk], writes=[qtk])
                        S.op("dve", I("scalar_tensor_tensor", out=khT[:, hh, :], in0=kT[:, hh, c0:c0 + 128], scalar=ebl[:, hh:hh + 1], in1=enb[:, hh, :], op0=ALU.mult, op1=ALU.mult),
                             reads=[kk[hh], enk, eblk], writes=[khk])
                        S.op("dve", I("tensor_scalar", out=qhat[:, hh, c0:c0 + 128], in0=qt[:, hh, :], scalar1=ePc[:, hh:hh + 1], scalar2=None, op0=ALU.mult),
                             reads=[qtk, epk], writes=[qhk])
                    for hp in range(2):
                        r0 = hp * 64
                        S.op("dve", I("tensor_tensor", out=kt[r0:r0 + 64, hp:4:2, :], in0=kT[r0:r0 + 64, :, c0:c0 + 128], in1=enb[r0:r0 + 64, :, :], op=ALU.mult),
                             reads=kk + [enk], writes=[ktk])
                    S.op("dve", I("tensor_tensor", out=ePc, in0=ePc, in1=ebl, op=ALU.mult), reads=[epk, eblk, qhk], writes=[epk])
                    btr = newps()
                    for hh in range(2):
                        S.op("pe", I("transpose", ps[btr][:, hh * 128:(hh + 1) * 128], khT[:, hh, :], identF), reads=[khk, "cst"], writes=[("ps", btr)])
                    S.op("act", I("copy", out=khtok, in_=ps[btr][:, 0:256]), reads=[("ps", btr)], writes=[khtk])
                    batt = newps()
                    for h in range(4):
                        S.op("pe", I("matmul", ps[batt][:, h * 128:(h + 1) * 128], lhsT=kt[:, h, :], rhs=qt[:, h // 2, :], start=True, stop=True),
                             reads=[ktk, qtk], writes=[("ps", batt)])
                    att, atk = ABa.alloc([4, 128], "att")
                    S.op("dve", I("tensor_tensor", out=att, in0=ps[batt][:, :].rearrange("p (h i) -> p h i", h=4), in1=triuF.unsqueeze(1).to_broadcast([128, 4, 128]), op=ALU.mult),
                         reads=[("ps", batt), "cst"], writes=[atk])
                    bo = newps()
                    for h in range(4):
                        p0, hh = (h % 2) * 64, h // 2
                        S.op("pe", I("matmul", ps[bo][:, h * 128:(h + 1) * 128], lhsT=vbf[:, h * 128:(h + 1) * 128], rhs=att[:, h, :], start=True, stop=False),
                             reads=[vbk, atk], writes=[("ps", bo)])
                        S.op("pe", I("matmul", ps[bo][:, h * 128:(h + 1) * 128], lhsT=Sbf[:, h, :], rhs=qt[:, hh, :], start=False, stop=True),
                             reads=[sbk, qtk], writes=[("ps", bo)])
                    S.op("act", I("copy", out=o_bf[:, :, c0:c0 + 128], in_=ps[bo][:, :].rearrange("p (h i) -> p h i", h=4)), reads=[("ps", bo)], writes=obk)
                    bs = newps()
                    for h in range(4):
                        hh = h // 2
                        S.op("pe", I("matmul", ps[bs][:, h * 128:(h + 1) * 128], lhsT=khtok[:, hh * 128:(hh + 1) * 128], rhs=vbf[:, h * 128:(h + 1) * 128], start=True, stop=True),
                             reads=[khtk, vbk], writes=[("ps", bs)])
                    for h in range(4):
                        p0, hh = (h % 2) * 64, h // 2
                        S.op("dve", I("scalar_tensor_tensor", out=Sst[p0:p0 + 64, hh, :], in0=Sst[p0:p0 + 64, hh, :], scalar=ebl[p0:p0 + 64, hh:hh + 1],
                                      in1=ps[bs][p0:p0 + 64, h * 128:(h + 1) * 128], op0=ALU.mult, op1=ALU.add),
                             reads=[ssk, eblk, ("ps", bs)], writes=[ssk])
                    for hp in range(2):
                        r0 = hp * 64
                        S.op("act", I("copy", out=Sbf[r0:r0 + 64, hp:4:2, :], in_=Sst[r0:r0 + 64, :, :]), reads=[ssk], writes=[sbk])
                else:
                    vsf, vsk = AFa.alloc([512], "vsf")
                    S.op("act", I("copy", out=vsf[0:NSMP, :], in_=ps[bv][0:NSMP, :]), reads=[("ps", bv)], writes=[vsk])
                    chk(321)
                    btq = newps()
                    for h in range(4):
                        S.op("pe", I("transpose", ps[btq][0:64, h * NSMP:(h + 1) * NSMP], kq[0:NSMP, 256 + h * 64:256 + (h + 1) * 64], identF[0:NSMP, 0:NSMP]),
                             reads=[kqk, "cst"], writes=[("ps", btq)])
                        S.op("pe", I("transpose", ps[btq][0:64, 64 + h * NSMP:64 + (h + 1) * NSMP], sp_[0:NSMP, h * 64:(h + 1) * 64], identF[0:NSMP, 0:NSMP]),
                             reads=[spk, "cst"], writes=[("ps", btq)])
                    chk(3211)
                    qd, qdk = AFa.alloc([4, NSMP], "qd")
                    ad, adk = AFa.alloc([4, NSMP], "ad")
                    S.op("dve", I("tensor_scalar", out=qd[0:64], in0=ps[btq][0:64, 0:64].rearrange("p (h s) -> p h s", h=4), scalar1=0.125, scalar2=None, op0=ALU.mult),
                         reads=[("ps", btq)], writes=[qdk])
                    chk(3212)
                    S.op("act", I("activation", out=ad[0:64], in_=ps[btq][0:64, 64:128].rearrange("p (h s) -> p h s", h=4), func=AF.Exp, scale=-1.0 / 16),
                         reads=[("ps", btq)], writes=[adk])
                    chk(322)
                    bso = [newps(), newps()]
                    reserved.update(bso)
                    Kms = [AFa.alloc([256], "Km%d" % i) for i in range(2)]
                    Sss = [AFa.alloc([4, 128], "Ss%d" % i) for i in range(2)]
                    for s in range(NSMP):
                        Km, kmk = Kms[s % 2]
                        Ss, sskey = Sss[s % 2]
                        S.dma("sp", I("dma_start", out=Ss[0:64], in_=sgl[l, s].rearrange("h d v -> d h v")), writes=[sskey])
                        S.op("dve", I("tensor_scalar", out=Km[0:NSMP], in0=kq[0:NSMP, 0:256], scalar1=identF[0:NSMP, s:s + 1], scalar2=None, op0=ALU.mult),
                             reads=[kqk, "cst"], writes=[kmk])
                        bk_ = newps()
                        for h in range(4):
                            S.op("pe", I("matmul", ps[bk_][0:64, h * 128:(h + 1) * 128], lhsT=Km[0:NSMP, h * 64:(h + 1) * 64], rhs=vsf[0:NSMP, h * 128:(h + 1) * 128], start=True, stop=True),
                                 reads=[kmk, vsk], writes=[("ps", bk_)])
                        for h in range(4):
                            S.op("dve", I("scalar_tensor_tensor", out=Ss[0:64, h, :], in0=Ss[0:64, h, :], scalar=ad[0:64, h, s:s + 1],
                                          in1=ps[bk_][0:64, h * 128:(h + 1) * 128], op0=ALU.mult, op1=ALU.add),
                                 reads=[sskey, adk, ("ps", bk_)], writes=[sskey])
                        for h in range(4):
                            cb = (h % 2) * 256 + s * NSMP
                            S.op("pe", I("matmul", ps[bso[h // 2]][:, cb:cb + NSMP], lhsT=Ss[0:64, h, :], rhs=qd[0:64, h, :], start=True, stop=True),
                                 reads=[sskey, qdk], writes=[("ps", bso[h // 2])])
                        S.dma("sp", I("dma_start", out=o_gl_s[l, s].rearrange("h d v -> d h v"), in_=Ss[0:64]), reads=[sskey], writes=["o_gl_s"])
                    for h in range(4):
                        cb = (h % 2) * 256
                        S.op("dve", I("tensor_copy", out=o_bf[:, h, TP:T], in_=ps[bso[h // 2]][:, cb:cb + 256:17]), reads=[("ps", bso[h // 2])], writes=[obk[h]])
                    reserved.difference_update(bso)
            chk(33)
            S.op("dve", I("tensor_copy", out=xs[:, XS_S:XS_S + 256], in_=Sst.rearrange("p a b -> p (a b)")), reads=[ssk], writes=["xs"])
            S.dma("sp", I("dma_start", out=cin[l].ap(), in_=xs[:]), reads=["xs"], writes=[("cin", l)])
            S.op("pool", lambda e, l=l: e.collective_compute("AllGather", ALU.bypass, replica_groups=[[2 * i, 2 * i + 1] for i in range(ncores // 2)],
                                                         ins=[cin[l].ap().opt()], outs=[cout[l].ap().opt()]),
                 reads=[("cin", l)], writes=[("cout", l)], tok=(ccsem[l], 1))
            S.barrier()
            AFa.release(mG_F); ABa.release(mG_B)

            chk(4)
            ug = mixT[:, 12:16, :]
            ugk = [("mix", 12 + h) for h in range(4)]
            for hb in range(2):
                su, ku = loadw(win_cols(l, C_DU + hb * 256, 256), 16, 256)
                for j in range(2):
                    cc = hb * 2 + j
                    b_ = mm_group(lambda k, j=j: su[:, k, j * 128:(j + 1) * 128], lambda k, t0, t1: hT[:, k, t0:t1], 16, [ku] + hkeys)
                    for bi, (t0, t1) in enumerate(TB):
                        S.op("act", I("activation", out=ug[:, cc, t0:t1], in_=ps[b_[bi]][:, 0:t1 - t0], func=AF.Gelu_apprx_tanh), reads=[("ps", b_[bi])], writes=[ugk[cc]])
            sd0, kd0 = loadw(win_cols(l, C_DV, 256), 16, 256)
            sd1, kd1 = loadw(win_cols(l, C_DV + 256, 256), 16, 256)
            mDt, mDtB = AFa.mark(), ABa.mark()
            dframes = [dict(vg=AFa.alloc([512], "vg"), junk=AFa.alloc([512], "junk"), st=AFa.alloc([8], "st"), svt=AFa.alloc([4, 128], "svt"), vdb=ABa.alloc([512], "vdb"))
                       for _ in range(2)]

            def d_stage_a(n):
                fr = dframes[n % 2]
                c0 = n * 128
                vg, vgk = fr["vg"]; junk, jk = fr["junk"]; st_, stk = fr["st"]; vdb, vdk = fr["vdb"]
                bv = newps()
                for half, (sv, kv) in enumerate(((sd0, kd0), (sd1, kd1))):
                    for k in range(16):
                        S.op("pe", I("matmul", ps[bv][:, half * 256:(half + 1) * 256], lhsT=hT[:, k, c0:c0 + 128], rhs=sv[:, k, :],
                                     start=(k == 0), stop=(k == 15)), reads=[kv, ("hT", k)], writes=[("ps", bv)])
                S.op("act", I("activation", out=vg, in_=ps[bv][:, :], func=AF.Gelu_apprx_tanh), reads=[("ps", bv)], writes=[vgk])
                S.op("act", I("activation", out=junk, in_=vg, func=AF.Square), reads=[vgk], writes=[jk])
                S.op("dve", I("tensor_reduce", out=st_[:, 0:1], in_=vg, axis=AX.X, op=ALU.add), reads=[vgk], writes=[stk])
                S.op("dve", I("tensor_reduce", out=st_[:, 1:2], in_=junk, axis=AX.X, op=ALU.add), reads=[jk, stk], writes=[stk])
                S.op("dve", I("tensor_scalar", out=st_[:, 2:3], in0=st_[:, 0:1], scalar1=1.0 / 512, scalar2=None, op0=ALU.mult), reads=[stk], writes=[stk])
                S.op("dve", I("tensor_tensor", out=st_[:, 3:4], in0=st_[:, 2:3], in1=st_[:, 2:3], op=ALU.mult), reads=[stk], writes=[stk])
                S.op("dve", I("scalar_tensor_tensor", out=st_[:, 4:5], in0=st_[:, 1:2], scalar=1.0 / 512, in1=st_[:, 3:4], op0=ALU.mult, op1=ALU.subtract), reads=[stk], writes=[stk])
                S.op("act", I("activation", out=st_[:, 5:6], in_=st_[:, 4:5], func=AF.Sqrt, bias=EPS, scale=1.0), reads=[stk], writes=[stk])
                S.op("dve", I("reciprocal", out=st_[:, 5:6], in_=st_[:, 5:6]), reads=[stk], writes=[stk])
                S.op("dve", I("tensor_scalar", out=vg, in0=vg, scalar1=st_[:, 2:3], scalar2=st_[:, 5:6], op0=ALU.subtract, op1=ALU.mult), reads=[vgk, stk], writes=[vgk])
                S.op("dve", I("tensor_tensor", out=vg, in0=vg, in1=bct[:, 0:512], op=ALU.mult), reads=[vgk, "bct"], writes=[vgk])
                S.op("dve", I("tensor_tensor", out=vg, in0=vg, in1=bct[:, 512:1024], op=ALU.add), reads=[vgk, "bct"], writes=[vgk])
                S.op("act", I("copy", out=vdb, in_=vg), reads=[vgk], writes=[vdk])
                if n == 7:
                    S.dma("sp", I("dma_start", out=o_sv_p[l], in_=vg), reads=[vgk], writes=["o_sv_p"])

            def d_stage_b(n):
                fr = dframes[n % 2]
                c0 = n * 128
                vdb, vdk = fr["vdb"]; svt, svk = fr["svt"]
                bsv = newps()
                for h in range(4):
                    S.op("pe", I("matmul", ps[bsv][:, h * 128:(h + 1) * 128], lhsT=vdb[:, h * 128:(h + 1) * 128], rhs=wsT[:, h, :], start=True, stop=True),
                         reads=[vdk, "wsT"], writes=[("ps", bsv)])
                S.op("dve", I("tensor_tensor", out=svt, in0=ps[bsv][:, :].rearrange("p (h i) -> p h i", h=4), in1=bct[:, 1280:1792].rearrange("p (h i) -> p h i", h=4), op=ALU.add),
                     reads=[("ps", bsv), "bct"], writes=[svk])
                S.op("dve", I("tensor_tensor", out=mixT[:, 12:16, c0:c0 + 128], in0=svt, in1=ug[:, :, c0:c0 + 128], op=ALU.mult),
                     reads=[svk] + ugk, writes=ugk)

            d_stage_a(0)
            for n in range(8):
                if n + 1 < 8:
                    d_stage_a(n + 1)
                d_stage_b(n)
            for n in (8,):
                if n == 8:
                    S.barrier()
                AFa.release(mDt); ABa.release(mDtB)
                c0 = n * 128
                m = 128 if n < 8 else NSMP
                bv = newps()
                for half, (sv, kv) in enumerate(((sd0, kd0), (sd1, kd1))):
                    for k in range(16):
                        S.op("pe", I("matmul", ps[bv][0:m, half * 256:(half + 1) * 256], lhsT=hT[:, k, c0:c0 + m], rhs=sv[:, k, :],
                                     start=(k == 0), stop=(k == 15)), reads=[kv, ("hT", k)], writes=[("ps", bv)])
                vg, vgk = AFa.alloc([512], "vg")
                junk, jk = AFa.alloc([512], "junk")
                st_, stk = AFa.alloc([8], "st")
                S.op("act", I("activation", out=vg[0:m], in_=ps[bv][0:m, :], func=AF.Gelu_apprx_tanh), reads=[("ps", bv)], writes=[vgk])
                S.op("act", I("activation", out=junk[0:m], in_=vg[0:m], func=AF.Square), reads=[vgk], writes=[jk])
                S.op("dve", I("tensor_reduce", out=st_[0:m, 0:1], in_=vg[0:m], axis=AX.X, op=ALU.add), reads=[vgk], writes=[stk])
                S.op("dve", I("tensor_reduce", out=st_[0:m, 1:2], in_=junk[0:m], axis=AX.X, op=ALU.add), reads=[jk, stk], writes=[stk])
                S.op("dve", I("tensor_scalar", out=st_[0:m, 2:3], in0=st_[0:m, 0:1], scalar1=1.0 / 512, scalar2=None, op0=ALU.mult), reads=[stk], writes=[stk])
                S.op("dve", I("tensor_tensor", out=st_[0:m, 3:4], in0=st_[0:m, 2:3], in1=st_[0:m, 2:3], op=ALU.mult), reads=[stk], writes=[stk])
                S.op("dve", I("scalar_tensor_tensor", out=st_[0:m, 4:5], in0=st_[0:m, 1:2], scalar=1.0 / 512, in1=st_[0:m, 3:4], op0=ALU.mult, op1=ALU.subtract), reads=[stk], writes=[stk])
                S.op("act", I("activation", out=st_[0:m, 5:6], in_=st_[0:m, 4:5], func=AF.Sqrt, bias=EPS, scale=1.0), reads=[stk], writes=[stk])
                S.op("dve", I("reciprocal", out=st_[0:m, 5:6], in_=st_[0:m, 5:6]), reads=[stk], writes=[stk])
                S.op("dve", I("tensor_scalar", out=vg[0:m], in0=vg[0:m], scalar1=st_[0:m, 2:3], scalar2=st_[0:m, 5:6], op0=ALU.subtract, op1=ALU.mult), reads=[vgk, stk], writes=[vgk])
                S.op("dve", I("tensor_tensor", out=vg[0:m], in0=vg[0:m], in1=bct[0:m, 0:512], op=ALU.mult), reads=[vgk, "bct"], writes=[vgk])
                S.op("dve", I("tensor_tensor", out=vg[0:m], in0=vg[0:m], in1=bct[0:m, 512:1024], op=ALU.add), reads=[vgk, "bct"], writes=[vgk])
                if n < 8:
                    vdb, vdk = ABa.alloc([512], "vdb")
                    S.op("act", I("copy", out=vdb, in_=vg), reads=[vgk], writes=[vdk])
                    if n == 7:
                        S.dma("sp", I("dma_start", out=o_sv_p[l], in_=vg), reads=[vgk], writes=["o_sv_p"])
                    bsv = newps()
                    for h in range(4):
                        S.op("pe", I("matmul", ps[bsv][:, h * 128:(h + 1) * 128], lhsT=vdb[:, h * 128:(h + 1) * 128], rhs=wsT[:, h, :], start=True, stop=True),
                             reads=[vdk, "wsT"], writes=[("ps", bsv)])
                    svt, svk = AFa.alloc([4, 128], "svt")
                    S.op("dve", I("tensor_tensor", out=svt, in0=ps[bsv][:, :].rearrange("p (h i) -> p h i", h=4), in1=bct[:, 1280:1792].rearrange("p (h i) -> p h i", h=4), op=ALU.add),
                         reads=[("ps", bsv), "bct"], writes=[svk])
                    S.op("dve", I("tensor_tensor", out=mixT[:, 12:16, c0:c0 + 128], in0=svt, in1=ug[:, :, c0:c0 + 128], op=ALU.mult),
                         reads=[svk] + ugk, writes=ugk)
                else:
                    S.dma("sp", I("dma_start", out=o_sv_s[l], in_=vg[0:NSMP]), reads=[vgk], writes=["o_sv_s"])
                    btv = newps()
                    for h in range(4):
                        S.op("pe", I("transpose", ps[btv][:, h * NSMP:(h + 1) * NSMP], vg[0:NSMP, h * 128:(h + 1) * 128], identF[0:NSMP, 0:NSMP]),
                             reads=[vgk, "cst"], writes=[("ps", btv)])
                    svs, svsk = AFa.alloc([4, NSMP], "svs")
                    for h in range(4):
                        S.op("dve", I("tensor_scalar", out=svs[:, h, :], in0=ps[btv][:, h * NSMP:(h + 1) * NSMP], scalar1=pv[:, P + PV_W00 + h:P + PV_W00 + h + 1],
                                      scalar2=pv[:, P + PV_B0 + h:P + PV_B0 + h + 1], op0=ALU.mult, op1=ALU.add), reads=[("ps", btv), "pv"], writes=[svsk])
                    S.op("dve", I("tensor_tensor", out=mixT[:, 12:16, TP:T], in0=svs, in1=ug[:, :, TP:T], op=ALU.mult),
                         reads=[svsk] + ugk, writes=ugk)
            S.barrier()
            AFa.release(mG_F); ABa.release(mG_B)

            chk(5)
            xr, xrk = AFa.alloc([2, XSN], "xr")
            S.dma("sp", I("dma_start", out=xr, in_=cout[l].ap().rearrange("(r p) x -> p r x", p=128)), reads=[("cout", l)], writes=[xrk])
            S.op("dve", I("tensor_scalar", out=xsel[:], in0=xr[:, 0, :], scalar1=wsel[:, 0:1], scalar2=None, op0=ALU.mult), reads=[xrk, "wsel"], writes=["xsel"])
            for r in range(1, 2):
                S.op("dve", I("scalar_tensor_tensor", out=xsel[:], in0=xr[:, r, :], scalar=wsel[:, r:r + 1], in1=xsel[:], op0=ALU.mult, op1=ALU.add),
                     reads=[xrk, "wsel", "xsel"], writes=["xsel"])
            S.barrier()
            AFa.release(mG_F); ABa.release(mG_B)
            fa, fak = AFa.alloc([4, 2], "fa")
            for cc in range(4):
                w0 = pv[:, P + PV_CAW + cc * 3 + 0:P + PV_CAW + cc * 3 + 1]
                w1 = pv[:, P + PV_CAW + cc * 3 + 1:P + PV_CAW + cc * 3 + 2]
                h0 = xsel[:, XS_A + cc * 2:XS_A + cc * 2 + 1]
                h1 = xsel[:, XS_A + cc * 2 + 1:XS_A + cc * 2 + 2]
                S.op("dve", I("scalar_tensor_tensor", out=fa[:, cc, 0:1], in0=h0, scalar=w0, in1=headA[:, cc, 0:1], op0=ALU.mult, op1=ALU.add), reads=["xsel", hak, "pv"], writes=[fak])
                S.op("dve", I("scalar_tensor_tensor", out=fa[:, cc, 0:1], in0=h1, scalar=w1, in1=fa[:, cc, 0:1], op0=ALU.mult, op1=ALU.add), reads=["xsel", fak], writes=[fak])
                S.op("dve", I("scalar_tensor_tensor", out=fa[:, cc, 1:2], in0=h1, scalar=w0, in1=headA[:, cc, 1:2], op0=ALU.mult, op1=ALU.add), reads=["xsel", hak], writes=[fak])
                S.op("dve", I("tensor_tensor", out=mixT[:, cc, 0:2], in0=fa[:, cc, :], in1=headA[:, cc, 2:4], op=ALU.mult), reads=[fak, hak], writes=[("mix", cc)])

            chk(6)
            sgr = mixT[:, 8:12, :]
            sgk = [("mix", 8 + h) for h in range(4)]
            for hb in range(2):
                sr, kr = loadw(win_cols(l, C_R + hb * 256, 256), 16, 256)
                for j in range(2):
                    cc = hb * 2 + j
                    b_ = mm_group(lambda k, j=j: sr[:, k, j * 128:(j + 1) * 128], lambda k, t0, t1: hT[:, k, t0:t1], 16, [kr] + hkeys)
                    for bi, (t0, t1) in enumerate(TB):
                        S.op("act", I("activation", out=sgr[:, cc, t0:t1], in_=ps[b_[bi]][:, 0:t1 - t0], func=AF.Silu), reads=[("ps", b_[bi])], writes=[sgk[cc]])
            Sib, sibk = ABa.alloc([4, 128], "Sibm")
            S.op("dve", I("memset", Sib, 0.0), writes=[sibk])
            for hp in range(2):
                r0 = hp * 64
                S.op("act", I("copy", out=Sib[r0:r0 + 64, hp:4:2, :], in_=xsel[r0:r0 + 64, XS_S:XS_S + 256].rearrange("p (a b) -> p a b", a=2)), reads=["xsel"], writes=[sibk])
            Sfin, sfk = AFa.alloc([2, 128], "Sfin")
            for hh in range(2):
                S.op("dve", I("scalar_tensor_tensor", out=Sfin[:, hh, :], in0=xsel[:, XS_S + hh * 128:XS_S + (hh + 1) * 128], scalar=ePc[:, hh:hh + 1], in1=Sst[:, hh, :],
                              op0=ALU.mult, op1=ALU.add), reads=["xsel", epk, ssk], writes=[sfk])
            S.dma("sp", I("dma_start", out=o_gl_p[l].rearrange("(hh p) v -> p hh v", p=128), in_=Sfin), reads=[sfk], writes=["o_gl_p"])
            fsets = [(AFa.alloc([512], "of%d" % i), ABa.alloc([512], "osq%d" % i), AFa.alloc([512], "rr%d" % i)) for i in range(3)]
            blocks = [(h, bi) for h in range(4) for bi in range(3)]
            bss_of = {}

            def stage_a(i):
                h, bi = blocks[i]
                hh = h // 2
                t0, t1 = TB[bi]
                w = t1 - t0
                (of, ofk), (osq, osk), (rr, rrk) = fsets[i % 3]
                if bi < 2:
                    bcq = newps()
                    S.op("pe", I("matmul", ps[bcq][:, 0:w], lhsT=Sib[:, h, :], rhs=qhat[:, hh, t0:t1], start=True, stop=True),
                         reads=[sibk, qhk], writes=[("ps", bcq)])
                    S.op("dve", I("tensor_tensor", out=of[:, 0:w], in0=ps[bcq][:, 0:w], in1=o_bf[:, h, t0:t1], op=ALU.add), reads=[("ps", bcq), obk[h]], writes=[ofk])
                else:
                    S.op("dve", I("tensor_copy", out=of[:, 0:w], in_=o_bf[:, h, t0:t1]), reads=[obk[h]], writes=[ofk])
                S.op("act", I("activation", out=osq[:, 0:w], in_=of[:, 0:w], func=AF.Square), reads=[ofk], writes=[osk])
                bss = newps()
                reserved.add(bss)
                bss_of[i] = bss
                S.op("pe", I("matmul", ps[bss][:, 0:w], lhsT=onesB[:], rhs=osq[:, 0:w], start=True, stop=True), reads=[osk, "ones"], writes=[("ps", bss)])

            def stage_b(i):
                h, bi = blocks[i]
                t0, t1 = TB[bi]
                w = t1 - t0
                (of, ofk), (osq, osk), (rr, rrk) = fsets[i % 3]
                bss = bss_of[i]
                S.op("act", I("activation", out=rr[:, 0:w], in_=ps[bss][:, 0:w], func=AF.Sqrt, bias=EPS, scale=1.0 / 128), reads=[("ps", bss)], writes=[rrk])
                reserved.discard(bss)
                S.op("dve", I("reciprocal", out=rr[:, 0:w], in_=rr[:, 0:w]), reads=[rrk], writes=[rrk])
                S.op("dve", I("scalar_tensor_tensor", out=of[:, 0:w], in0=of[:, 0:w], scalar=pv[:, P + PV_GNORM + h:P + PV_GNORM + h + 1], in1=rr[:, 0:w], op0=ALU.mult, op1=ALU.mult),
                     reads=[ofk, rrk, "pv"], writes=[ofk])
                S.op("dve", I("tensor_tensor", out=mixT[:, 4 + h, t0:t1], in0=of[:, 0:w], in1=sgr[:, h, t0:t1], op=ALU.mult), reads=[ofk, sgk[h]], writes=[obk[h]])

            for i in range(len(blocks) + 1):
                if i < len(blocks):
                    stage_a(i)
                if i >= 1:
                    stage_b(i - 1)
            S.barrier()
            AFa.release(mP_F); ABa.release(mP_B)

            chk(7)
            for cc in range(4):
                S.op("act", I("copy", out=extC[:, cc, 0:30], in_=xsel[:, XS_C + cc * 30:XS_C + cc * 30 + 30]), reads=["xsel"], writes=[eck])
            xcs, xcsk = AFa.alloc([4, NSMP], "xcs")
            mCs = AFa.mark()
            hisC, hck = AFa.alloc([4, NSMP, 30], "hisC2")
            accs, ask_ = AFa.alloc([NSMP, 30], "accs")
            S.dma("sp", I("dma_start", out=hisC, in_=scc[l]), writes=[hck])
            for cc in range(4):
                S.op("dve", I("tensor_tensor", out=accs, in0=hisC[:, cc, :, :], in1=pv[:, P + PV_CCW + cc * 31:P + PV_CCW + cc * 31 + 30].unsqueeze(1).to_broadcast([128, NSMP, 30]), op=ALU.mult),
                     reads=[hck, "pv"], writes=[ask_])
                S.op("dve", I("tensor_reduce", out=xcs[:, cc, :], in_=accs, axis=AX.X, op=ALU.add), reads=[ask_], writes=[xcsk])
            S.barrier()
            AFa.release(mCs)
            xc, _ = AFa.alloc([4, T], "xc")
            xck = [("xc", c_) for c_ in range(4)]
            xb = [ABa.alloc([T], "xb%d" % i) for i in range(1)] * 2
            sqb = [ABa.alloc([T], "sqb%d" % i) for i in range(1)] * 2
            dg = [ABa.alloc([128], "dg%d" % i) for i in range(4)]
            b1 = [newps() for _ in TB]
            b2 = [newps() for _ in TB]
            reserved.update(b1); reserved.update(b2)
            for cc in range(4):
                wc = lambda j, cc=cc: pv[:, P + PV_CCW + cc * 31 + j:P + PV_CCW + cc * 31 + j + 1]
                bcol = pv[:, P + PV_CCB + cc:P + PV_CCB + cc + 1]
                cb = [newps(), newps()]
                for j in range(31):
                    dga, dgk = dg[(cc * 31 + j) % 4]
                    S.op("dve", I("tensor_scalar", out=dga, in0=identB, scalar1=wc(j), scalar2=None, op0=ALU.mult), reads=["cstb", "pv"], writes=[dgk])
                    for bi in range(2):
                        t0, t1 = TB[bi]
                        S.op("pe", I("matmul", ps[cb[bi]][:, 0:512], lhsT=dga, rhs=extC[:, cc, t0 + j:t1 + j], start=(j == 0), stop=(j == 30)),
                             reads=[dgk, eck], writes=[("ps", cb[bi])])
                for bi in range(2):
                    t0, t1 = TB[bi]
                    S.op("act", I("activation", out=xc[:, cc, t0:t1], in_=ps[cb[bi]][:, 0:512], func=AF.Identity, bias=bcol, scale=1.0), reads=[("ps", cb[bi]), "pv"], writes=[xck[cc]])
                S.op("dve", I("scalar_tensor_tensor", out=xc[:, cc, TP:T], in0=extC[:, cc, 30 + TP:30 + T], scalar=wc(30), in1=xcs[:, cc, :], op0=ALU.mult, op1=ALU.add),
                     reads=[eck, xcsk], writes=[xck[cc]])
                S.op("dve", I("tensor_scalar", out=xc[:, cc, TP:T], in0=xc[:, cc, TP:T], scalar1=bcol, scalar2=None, op0=ALU.add), reads=[xck[cc], "pv"], writes=[xck[cc]])
                xba, xbk = xb[cc % 2]
                sqa, sqk = sqb[cc % 2]
                S.op("act", I("copy", out=xba, in_=xc[:, cc, :]), reads=[xck[cc]], writes=[xbk])
                S.op("act", I("activation", out=sqa, in_=xc[:, cc, :], func=AF.Square), reads=[xck[cc]], writes=[sqk])
                for bi, (t0, t1) in enumerate(TB):
                    S.op("pe", I("matmul", ps[b1[bi]][:, 0:t1 - t0], lhsT=onesB[:], rhs=xba[:, t0:t1], start=(cc == 0), stop=(cc == 3)), reads=[xbk, "ones"], writes=[("ps", b1[bi])])
                    S.op("pe", I("matmul", ps[b2[bi]][:, 0:t1 - t0], lhsT=onesB[:], rhs=sqa[:, t0:t1], start=(cc == 0), stop=(cc == 3)), reads=[sqk, "ones"], writes=[("ps", b2[bi])])
            reserved.difference_update(b1); reserved.difference_update(b2)
            mean, mnk = AFa.alloc([512], "mean")
            rstd, rsk2 = AFa.alloc([512], "rstd")
            for bi, (t0, t1) in enumerate(TB):
                w = t1 - t0
                S.op("dve", I("tensor_scalar", out=mean[:, 0:w], in0=ps[b1[bi]][:, 0:w], scalar1=1.0 / 512, scalar2=None, op0=ALU.mult), reads=[("ps", b1[bi])], writes=[mnk])
                S.op("dve", I("tensor_tensor", out=rstd[:, 0:w], in0=mean[:, 0:w], in1=mean[:, 0:w], op=ALU.mult), reads=[mnk], writes=[rsk2])
                S.op("dve", I("scalar_tensor_tensor", out=rstd[:, 0:w], in0=ps[b2[bi]][:, 0:w], scalar=1.0 / 512, in1=rstd[:, 0:w], op0=ALU.mult, op1=ALU.subtract),
                     reads=[("ps", b2[bi]), rsk2], writes=[rsk2])
                S.op("act", I("activation", out=rstd[:, 0:w], in_=rstd[:, 0:w], func=AF.Sqrt, bias=EPS, scale=1.0), reads=[rsk2], writes=[rsk2])
                S.op("dve", I("reciprocal", out=rstd[:, 0:w], in_=rstd[:, 0:w]), reads=[rsk2], writes=[rsk2])
                for cc in range(4):
                    S.op("dve", I("tensor_tensor", out=xc[:, cc, t0:t1], in0=xc[:, cc, t0:t1], in1=mean[:, 0:w], op=ALU.subtract), reads=[xck[cc], mnk], writes=[xck[cc]])
                    S.op("dve", I("tensor_tensor", out=xc[:, cc, t0:t1], in0=xc[:, cc, t0:t1], in1=rstd[:, 0:w], op=ALU.mult), reads=[xck[cc], rsk2], writes=[xck[cc]])
                    S.op("act", I("activation", out=mixT[:, 8 + cc, t0:t1], in_=xc[:, cc, t0:t1], func=AF.Silu, scale=pv[:, P + PV_LCG + cc:P + PV_LCG + cc + 1],
                                  bias=pv[:, P + PV_LCB + cc:P + PV_LCB + cc + 1]), reads=[xck[cc], "pv"], writes=[("mix", 8 + cc)])
            S.barrier()
            AFa.release(mL_F); ABa.release(mL_B)

            chk(8)
            for blk in range(8):
                so, ko = loadw(w_o[l, :, blk * 256:(blk + 1) * 256], 16, 256)
                for j in range(2):
                    d = blk * 2 + j
                    b_ = mm_group(lambda k, j=j: so[:, k, j * 128:(j + 1) * 128], lambda k, t0, t1: mixT[:, k, t0:t1], 16, [ko] + mkeys, tbs=TBF)
                    for bi, (t0, t1) in enumerate(TBF):
                        S.op("dve", I("tensor_tensor", out=xT[:, d, t0:t1], in0=ps[b_[bi]][:, 0:t1 - t0], in1=xT[:, d, t0:t1], op=ALU.add),
                             reads=[("ps", b_[bi]), ("xT", d)], writes=[("xT", d)])
            chk(9)
            rs, rsk, mk = rmsnorm_to_hT(P + PV_NFFN, "n2")
            norm_apply(rs, rsk, P + PV_NFFN)
            S.barrier()
            AFa.release(mk[0]); ABa.release(mk[1])
            mFF = AFa.mark()
            sgt = [AFa.alloc([T], "sgt%d" % i) for i in range(2)]
            actb = [(mixT[:, 4 * i:4 * i + 4, :], ("act", i)) for i in range(2)]
            for g in range(11):
                act_, actk = actb[g % 2]
                for half in range(2):
                    sg_, kg = loadw(w_gate[l, :, g * 512 + half * 256:g * 512 + half * 256 + 256], 16, 256)
                    su_, ku = loadw(w_up[l, :, g * 512 + half * 256:g * 512 + half * 256 + 256], 16, 256)
                    for j in range(2):
                        fc = half * 2 + j
                        sga, sgk2 = sgt[fc % 2]
                        bg = mm_group(lambda k, j=j: sg_[:, k, j * 128:(j + 1) * 128], lambda k, t0, t1: hT[:, k, t0:t1], 16, [kg] + hkeys, tbs=TBF)
                        for bi, (t0, t1) in enumerate(TBF):
                            S.op("act", I("activation", out=sga[:, t0:t1], in_=ps[bg[bi]][:, 0:t1 - t0], func=AF.Silu), reads=[("ps", bg[bi])], writes=[sgk2])
                        bu = mm_group(lambda k, j=j: su_[:, k, j * 128:(j + 1) * 128], lambda k, t0, t1: hT[:, k, t0:t1], 16, [ku] + hkeys, tbs=TBF)
                        for bi, (t0, t1) in enumerate(TBF):
                            S.op("dve", I("tensor_tensor", out=act_[:, fc, t0:t1], in0=ps[bu[bi]][:, 0:t1 - t0], in1=sga[:, t0:t1], op=ALU.mult),
                                 reads=[("ps", bu[bi]), sgk2], writes=[actk])
                for half in range(2):
                    sd_, kd = loadw(w_down[l, g * 512:(g + 1) * 512, half * 1024:(half + 1) * 1024], 4, 1024)
                    for j in range(8):
                        d = half * 8 + j
                        b_ = mm_group(lambda k, j=j: sd_[:, k, j * 128:(j + 1) * 128], lambda k, t0, t1: act_[:, k, t0:t1], 4, [kd, actk], tbs=TBF)
                        for bi, (t0, t1) in enumerate(TBF):
                            S.op("dve", I("tensor_tensor", out=xT[:, d, t0:t1], in0=ps[b_[bi]][:, 0:t1 - t0], in1=xT[:, d, t0:t1], op=ALU.add),
                                 reads=[("ps", b_[bi]), ("xT", d)], writes=[("xT", d)])
            S.barrier()
            AFa.release(mFF)

        chk(10)
        rs, rsk, mk = rmsnorm_to_hT(PV_NFIN, "nf")
        yb = [AFa.alloc([T], "yb%d" % i) for i in range(2)]
        for c in range(16):
            ya, yk = yb[c % 2]
            S.op("dve", I("scalar_tensor_tensor", out=ya, in0=xT[:, c, :], scalar=pv[:, PV_NFIN + c:PV_NFIN + c + 1], in1=rs, op0=ALU.mult, op1=ALU.mult),
                 reads=[("xT", c), rsk, "pv"], writes=[yk])
            S.dma("sp", I("dma_start", out=yT[:, c, :], in_=ya), reads=[yk], writes=[("yT", c)])
        S.emit(block)
    return nc


def _host_layout(inp, core):
    f = np.float32
    b, half = core // 2, core % 2
    s0 = core * NSMP
    xp = inp["x_prompt"][b, half * TP:(half + 1) * TP]
    xsm = inp["x_sample"][s0:s0 + NSMP, 0]
    xtok = np.concatenate([xp, xsm], axis=0)
    xT = np.ascontiguousarray(xtok.reshape(T, 16, 128).transpose(2, 1, 0))
    sca = np.ascontiguousarray(inp["state_conv_a"][:, s0:s0 + NSMP].reshape(2, NSMP, 2, 4, 128).transpose(0, 4, 3, 2, 1))
    sgl = np.ascontiguousarray(inp["state_gla"][:, s0:s0 + NSMP])
    scc = np.ascontiguousarray(inp["state_conv_c"][:, s0:s0 + NSMP].reshape(2, NSMP, 30, 4, 128).transpose(0, 4, 3, 1, 2))
    wsel = np.zeros((128, 8), f)
    if half == 1:
        wsel[:, 0] = 1.0
    return dict(xT_in=xT, sca=sca, sgl=sgl, scc=scc, wsel=wsel)


def _shared_layout(inp):
    f = np.float32
    pv = np.zeros((128, NPV), f)

    def fm(v, nchunk):
        return np.asarray(v, f).reshape(nchunk, 128).T

    for l in range(2):
        P = l * PVL
        pv[:, P + PV_NMIX:P + PV_NMIX + 16] = fm(inp["norm_mix"][l], 16)
        pv[:, P + PV_NFFN:P + PV_NFFN + 16] = fm(inp["norm_ffn"][l], 16)
        pv[:, P + PV_CAW:P + PV_CAW + 12] = np.asarray(inp["conv_a_w"][l], f).reshape(3, 4, 128).transpose(2, 1, 0).reshape(128, 12)
        pv[:, P + PV_GNORM:P + PV_GNORM + 4] = fm(inp["gla_norm"][l], 4)
        pv[:, P + PV_CCW:P + PV_CCW + 124] = np.asarray(inp["conv_c_w"][l], f).reshape(31, 4, 128).transpose(2, 1, 0).reshape(128, 124)
        pv[:, P + PV_CCB:P + PV_CCB + 4] = fm(inp["conv_c_b"][l], 4)
        pv[:, P + PV_LCG:P + PV_LCG + 4] = fm(inp["ln_c_g"][l], 4)
        pv[:, P + PV_LCB:P + PV_LCB + 4] = fm(inp["ln_c_b"][l], 4)
        pv[:, P + PV_W00:P + PV_W00 + 4] = np.broadcast_to(np.asarray(inp["sg_w"][l, :, 0, 0], f)[None, :], (128, 4))
        pv[:, P + PV_B0:P + PV_B0 + 4] = np.broadcast_to(np.asarray(inp["sg_b"][l, :, 0], f)[None, :], (128, 4))
    pv[:, PV_NFIN:PV_NFIN + 16] = fm(inp["norm_final"], 16)
    bc = np.zeros((2, 128, 1792), f)
    for l in range(2):
        row = np.concatenate([inp["ln_d_g"][l], inp["ln_d_b"][l], inp["gla_a_bias"][l], np.asarray(inp["sg_b"][l], f).reshape(-1)])
        bc[l] = np.broadcast_to(np.asarray(row, f)[None, :], (128, 1792))
    sgwT = np.ascontiguousarray(np.asarray(inp["sg_w"], f).transpose(0, 3, 1, 2))
    consts = np.zeros((128, 256), f)
    consts[:, 0:128] = np.eye(128, dtype=f)
    consts[:, 128:256] = np.triu(np.ones((128, 128), f))
    return dict(pvec=pv, bc_in=bc, sgwT=sgwT, a2_in=np.ascontiguousarray(inp["gla_a2"], dtype=f), consts=consts,
                w_in=np.ascontiguousarray(inp["w_in"], dtype=f), w_o=np.ascontiguousarray(inp["w_o"], dtype=f),
                w_gate=np.ascontiguousarray(inp["w_gate"], dtype=f), w_up=np.ascontiguousarray(inp["w_up"], dtype=f),
                w_down=np.ascontiguousarray(inp["w_down"], dtype=f))


def kernel(**inputs):
    inp = {k: np.asarray(v) for k, v in inputs.items()}
    shared = _shared_layout(inp)
    in_maps = []
    for c in range(NCORES):
        m = dict(shared)
        m.update(_host_layout(inp, c))
        in_maps.append(m)
    import os
    nc = build_nc(NCORES, int(os.environ.get('KSTOP_DBG', '0')))
    res = run_bass_kernel_spmd(nc, in_maps, core_ids=list(range(NCORES)))
    R = res.results
    f = np.float32
    y_prompt = np.zeros((4, 2048, D), f)
    y_sample = np.zeros((128, 1, D), f)
    ca_p = np.zeros((2, 4, 2, 512), f); ca_s = np.zeros((2, 128, 2, 512), f)
    gl_p = np.zeros((2, 4, 4, 64, 128), f); gl_s = np.zeros((2, 128, 4, 64, 128), f)
    cc_p = np.zeros((2, 4, 30, 512), f); cc_s = np.zeros((2, 128, 30, 512), f)
    sv_p = np.zeros((2, 4, 128, 512), f); sv_s = np.zeros((2, 128, 1, 512), f)
    for c in range(NCORES):
        r = R[c]
        b, half = c // 2, c % 2
        s0 = c * NSMP
        ytok = np.asarray(r["yT"]).transpose(2, 1, 0).reshape(T, D)
        y_prompt[b, half * TP:(half + 1) * TP] = ytok[:TP]
        y_sample[s0:s0 + NSMP, 0] = ytok[TP:]
        ca_s[:, s0:s0 + NSMP] = np.asarray(r["o_ca_s"]).transpose(0, 4, 3, 2, 1).reshape(2, NSMP, 2, 512)
        gl_s[:, s0:s0 + NSMP] = np.asarray(r["o_gl_s"])
        cc_s[:, s0:s0 + NSMP] = np.asarray(r["o_cc_s"]).transpose(0, 3, 4, 2, 1).reshape(2, NSMP, 30, 512)
        sv_s[:, s0:s0 + NSMP, 0] = np.asarray(r["o_sv_s"])
        if half == 1:
            ca_p[:, b] = np.asarray(r["o_ca_p"]).transpose(0, 3, 2, 1).reshape(2, 2, 512)
            gl_p[:, b] = np.asarray(r["o_gl_p"]).reshape(2, 4, 64, 128)
            cc_p[:, b] = np.asarray(r["o_cc_p"]).transpose(0, 3, 2, 1).reshape(2, 30, 512)
            sv_p[:, b] = np.asarray(r["o_sv_p"])
    return (y_prompt, y_sample, ca_p, ca_s, gl_p, gl_s, cc_p, cc_s, sv_p, sv_s)
```
